# Optimizing a Trainium2 kernel written in Bass

```python
import math
import jax, jax.numpy as jnp
from jax import lax
import numpy as np

D_MODEL = 1024
BATCH = 16
SEQ = 2048
DEPTH = 4

D_FF = 2816
FFN_RES_WEIGHT = 0.5
CONV_A_CH = 512
CONV_A_WIDTH = 31
CONV_B_CH = 512
CONV_B_WIDTH = 3
NSA_HEADS = 16
NSA_KV_GROUPS = 1
NSA_HEADS_PER_GROUP = NSA_HEADS // NSA_KV_GROUPS
HEAD_DIM = 64
CMP_BLOCK = 32
CMP_STRIDE = 16
CMP_HIDDEN = 128
SLC_BLOCK = 64
SLC_TOPK = 16
WINDOW = 512
Q_BLOCK = 128
POOL_WINDOWS = (2, 4, 8, 16)
POOL_GROUP_CH = 64
POOL_CH = len(POOL_WINDOWS) * POOL_GROUP_CH
REL_BUCKETS = 32
REL_MAX_DIST = 128
DN_ALPHA = (2 * DEPTH) ** 0.25
DN_BETA = (8 * DEPTH) ** -0.25

N_EVEN = (DEPTH + 1) // 2
N_ODD = DEPTH // 2
EVEN_IN = 2 * CONV_A_CH + 3 * CONV_B_CH
EVEN_OUT = CONV_A_CH + CONV_B_CH
NSA_Q = NSA_HEADS * HEAD_DIM
KV_W = NSA_KV_GROUPS * HEAD_DIM
NSA_GATES = 3 * NSA_HEADS
ODD_IN = NSA_Q + 6 * KV_W + NSA_GATES + POOL_CH
ODD_OUT = NSA_Q + POOL_CH
NEG_INF = -1e30
FORCE = 1e30

kernel_name = 'hybrid_conv_nsa_pool_macaron'


def layer_norm(x, g, b, eps=1e-5):
    xf = x.astype(jnp.float32)
    mu = xf.mean(-1, keepdims=True)
    var = jnp.square(xf - mu).mean(-1, keepdims=True)
    y = (xf - mu) * lax.rsqrt(var + eps)
    return (y * g.astype(jnp.float32) + b.astype(jnp.float32)).astype(x.dtype)


def swiglu(h, w_gate, w_up, w_down):
    return (jax.nn.silu(h @ w_gate) * (h @ w_up)) @ w_down


def adaln(x, m):
    return x * (1 + m[:, 1]) + m[:, 0]


def deepnorm_update(x, y, m, res_w, g, b):
    return layer_norm(DN_ALPHA * x + res_w * (1 + m[:, 2]) * y, g, b)


def causal_dwconv(x, w):
    k, ch = w.shape
    return lax.conv_general_dilated(x, w[:, None, :].astype(x.dtype), (1,), [(k - 1, 0)],
                                    dimension_numbers=('NWC', 'WIO', 'NWC'),
                                    feature_group_count=ch)


def t5_bucket(dist):
    n = jnp.maximum(dist, 0)
    exact = REL_BUCKETS // 2
    nf = jnp.maximum(n, 1).astype(jnp.float32)
    large = exact + (jnp.log(nf / exact) / math.log(REL_MAX_DIST / exact)
                     * (REL_BUCKETS - exact)).astype(jnp.int32)
    return jnp.where(n < exact, n, jnp.minimum(large, REL_BUCKETS - 1))


def masked_softmax(logits, valid):
    p = jax.nn.softmax(jnp.where(valid, logits, NEG_INF), axis=-1)
    return p * jnp.any(valid, axis=-1, keepdims=True)


def even_mixer(h, w_in, conv_a_w, conv_a_b, norm_a_g, norm_a_b, conv_b_w, w_out):
    p = h @ w_in
    a_val, a_gate, gate_b, gate_c, b_in = jnp.split(
        p, [CONV_A_CH, 2 * CONV_A_CH, 2 * CONV_A_CH + CONV_B_CH, 2 * CONV_A_CH + 2 * CONV_B_CH], axis=-1)
    u = a_val * jax.nn.sigmoid(a_gate)
    u = causal_dwconv(u, conv_a_w) + conv_a_b
    u = jax.nn.silu(layer_norm(u, norm_a_g, norm_a_b))
    z = gate_b * causal_dwconv(gate_c * b_in, conv_b_w)
    return jnp.concatenate([u, z], axis=-1) @ w_out


def pool_mixer(u, pool_w, pool_scale):
    bsz, s, _ = u.shape
    uf = u.astype(jnp.float32).reshape(bsz, s, len(POOL_WINDOWS), POOL_GROUP_CH)
    cs = jnp.pad(jnp.cumsum(uf, axis=1), ((0, 0), (1, 0), (0, 0), (0, 0)))
    t = jnp.arange(s)
    diffs = []
    for gi, w in enumerate(POOL_WINDOWS):
        lo = jnp.maximum(t + 1 - w, 0)
        cnt = (t + 1 - lo).astype(jnp.float32)
        cs_g = cs[:, :, gi]
        mean = (cs_g[:, t + 1] - cs_g[:, lo]) / cnt[None, :, None]
        diffs.append(mean - uf[:, :, gi])
    d = jnp.stack(diffs, axis=2).astype(u.dtype)
    y = jnp.einsum('bsgc,gcd->bsgd', d, pool_w).reshape(bsz, s, POOL_CH)
    return y * pool_scale


def nsa_attention(q, gates, k_cmp, v_cmp, k_slc, v_slc, k_win, v_win,
                  pe_k, pe_v, w1_k, w2_k, w1_v, w2_v, rel_bias):
    bsz, s = q.shape[:2]
    G, HG, Dh = NSA_KV_GROUPS, NSA_HEADS_PER_GROUP, HEAD_DIM
    dt = q.dtype
    scale = Dh ** -0.5
    tb = rel_bias.astype(jnp.float32).reshape(REL_BUCKETS, G, HG)

    n_cmp = (s - CMP_BLOCK) // CMP_STRIDE + 1
    cmp_start = np.arange(n_cmp) * CMP_STRIDE
    tok_idx = cmp_start[:, None] + np.arange(CMP_BLOCK)[None, :]

    def compress(kv, pe, w1, w2):
        blk = kv[:, tok_idx] + pe[None, None, :, None, :]
        blk = jnp.moveaxis(blk, 3, 2).reshape(bsz, n_cmp, G, CMP_BLOCK * Dh)
        return jax.nn.silu(blk @ w1) @ w2

    kc = compress(k_cmp, pe_k, w1_k, w2_k)
    vc = compress(v_cmp, pe_v, w1_v, w2_v)
    cmp_end = jnp.asarray(cmp_start + CMP_BLOCK - 1, jnp.int32)

    n_slc = s // SLC_BLOCK
    slc_start = np.arange(n_slc) * SLC_BLOCK
    ov = np.clip(np.minimum(cmp_start[:, None] + CMP_BLOCK, slc_start[None, :] + SLC_BLOCK)
                 - np.maximum(cmp_start[:, None], slc_start[None, :]), 0, None) / CMP_BLOCK
    overlap = jnp.asarray(ov, jnp.float32)
    n_top = min(SLC_TOPK, n_slc)
    ks_blk = jnp.moveaxis(k_slc.reshape(bsz, n_slc, SLC_BLOCK, G, Dh), 3, 1)
    vs_blk = jnp.moveaxis(v_slc.reshape(bsz, n_slc, SLC_BLOCK, G, Dh), 3, 1)

    kw_pad = jnp.pad(k_win, ((0, 0), (WINDOW, 0), (0, 0), (0, 0)))
    vw_pad = jnp.pad(v_win, ((0, 0), (WINDOW, 0), (0, 0), (0, 0)))
    n_win = WINDOW + Q_BLOCK

    n_qb = s // Q_BLOCK
    qb = q.reshape(bsz, n_qb, Q_BLOCK, G, HG, Dh).transpose(1, 0, 3, 4, 2, 5)
    gb = gates.reshape(bsz, n_qb, Q_BLOCK, G, HG, 3).transpose(1, 0, 3, 4, 2, 5)
    bidx = jnp.arange(bsz)[:, None, None, None]
    gidx = jnp.arange(G)[None, :, None, None]
    blk_ids = jnp.arange(n_slc, dtype=jnp.int32)

    def head_bias(dist):
        return tb[t5_bucket(dist)].transpose(2, 3, 0, 1)

    def query_block(args):
        i, q_i, g_i = args
        t0 = i * Q_BLOCK
        tq = t0 + jnp.arange(Q_BLOCK, dtype=jnp.int32)
        valid_c = cmp_end[None, :] <= tq[:, None]
        l_c = (jnp.einsum('bghtd,bngd->bghtn', q_i, kc).astype(jnp.float32) * scale
               + head_bias(tq[:, None] - cmp_end[None, :]))
        p_c = masked_softmax(l_c, valid_c)
        o_c = jnp.einsum('bghtn,bngd->bghtd', p_c.astype(dt), vc)
        imp = jnp.einsum('bghtn,ns->bgts', p_c, overlap)
        cur = tq // SLC_BLOCK
        forced = ((blk_ids[None, :] == 0) | (blk_ids[None, :] == cur[:, None])
                  | (blk_ids[None, :] == cur[:, None] - 1))
        causal = blk_ids[None, :] <= cur[:, None]
        imp = jnp.where(forced, FORCE, jnp.where(causal, imp, -FORCE))
        _, sel = lax.top_k(imp, n_top)
        k_s = ks_blk[bidx, gidx, sel].reshape(bsz, G, Q_BLOCK, n_top * SLC_BLOCK, Dh)
        v_s = vs_blk[bidx, gidx, sel].reshape(bsz, G, Q_BLOCK, n_top * SLC_BLOCK, Dh)
        pos = (sel[..., None] * SLC_BLOCK + jnp.arange(SLC_BLOCK, dtype=jnp.int32)).reshape(
            bsz, G, Q_BLOCK, n_top * SLC_BLOCK)
        dist_s = tq[None, None, :, None] - pos
        bias_s = jnp.moveaxis(tb[t5_bucket(dist_s), gidx], -1, 2)
        l_s = jnp.einsum('bghtd,bgtkd->bghtk', q_i, k_s).astype(jnp.float32) * scale + bias_s
        p_s = masked_softmax(l_s, (dist_s >= 0)[:, :, None])
        o_s = jnp.einsum('bghtk,bgtkd->bghtd', p_s.astype(dt), v_s)
        k_w = lax.dynamic_slice_in_dim(kw_pad, t0, n_win, axis=1)
        v_w = lax.dynamic_slice_in_dim(vw_pad, t0, n_win, axis=1)
        kpos = t0 - WINDOW + jnp.arange(n_win, dtype=jnp.int32)
        dist_w = tq[:, None] - kpos[None, :]
        valid_w = (dist_w >= 0) & (dist_w < WINDOW) & (kpos[None, :] >= 0)
        l_w = (jnp.einsum('bghtd,blgd->bghtl', q_i, k_w).astype(jnp.float32) * scale
               + head_bias(dist_w))
        p_w = masked_softmax(l_w, valid_w)
        o_w = jnp.einsum('bghtl,blgd->bghtd', p_w.astype(dt), v_w)
        return g_i[..., 0:1] * o_c + g_i[..., 1:2] * o_s + g_i[..., 2:3] * o_w

    out = lax.map(query_block, (jnp.arange(n_qb, dtype=jnp.int32), qb, gb))
    return out.transpose(1, 0, 4, 2, 3, 5).reshape(bsz, s, NSA_HEADS * Dh)


def odd_mixer(h, w_in, pe_k, pe_v, w1_k, w2_k, w1_v, w2_v, pool_w, pool_scale, w_out, rel_bias):
    bsz, s, _ = h.shape
    p = h @ w_in
    splits = [int(v) for v in np.cumsum([NSA_Q] + [KV_W] * 6 + [NSA_GATES])]
    q, kc, vc, ks, vs, kw, vw, g, u = jnp.split(p, splits, axis=-1)
    kvs = (bsz, s, NSA_KV_GROUPS, HEAD_DIM)
    o_nsa = nsa_attention(q.reshape(bsz, s, NSA_HEADS, HEAD_DIM),
                          jax.nn.sigmoid(g).reshape(bsz, s, NSA_HEADS, 3),
                          kc.reshape(kvs), vc.reshape(kvs), ks.reshape(kvs), vs.reshape(kvs),
                          kw.reshape(kvs), vw.reshape(kvs),
                          pe_k, pe_v, w1_k, w2_k, w1_v, w2_v, rel_bias)
    o_pool = pool_mixer(u, pool_w, pool_scale)
    return jnp.concatenate([o_nsa, o_pool], axis=-1) @ w_out


def setup_inputs(seed: int = 0) -> dict:
    key = jax.random.key(seed)
    keys = iter(jax.random.split(key, 40))

    def nrm(shape, scale):
        return jax.random.normal(next(keys), shape, jnp.float32) * scale

    D = D_MODEL
    return {
        'x': nrm((BATCH, SEQ, D), 1.0),
        'c': nrm((BATCH, D), 1.0),
        'ada_w': nrm((DEPTH, D, 9 * D), 0.2 * D ** -0.5),
        'ada_b': nrm((DEPTH, 9 * D), 0.02),
        'ln_g': 1.0 + nrm((DEPTH, 3, D), 0.02),
        'ln_b': nrm((DEPTH, 3, D), 0.02),
        'ffn_w_gate': nrm((DEPTH, 2, D, D_FF), D ** -0.5),
        'ffn_w_up': nrm((DEPTH, 2, D, D_FF), D ** -0.5),
        'ffn_w_down': nrm((DEPTH, 2, D_FF, D), DN_BETA * D_FF ** -0.5),
        'ev_w_in': nrm((N_EVEN, D, EVEN_IN), D ** -0.5),
        'ev_conv_a_w': nrm((N_EVEN, CONV_A_WIDTH, CONV_A_CH), CONV_A_WIDTH ** -0.5),
        'ev_conv_a_b': nrm((N_EVEN, CONV_A_CH), 0.02),
        'ev_norm_a_g': 1.0 + nrm((N_EVEN, CONV_A_CH), 0.02),
        'ev_norm_a_b': nrm((N_EVEN, CONV_A_CH), 0.02),
        'ev_conv_b_w': nrm((N_EVEN, CONV_B_WIDTH, CONV_B_CH), CONV_B_WIDTH ** -0.5),
        'ev_w_out': nrm((N_EVEN, EVEN_OUT, D), DN_BETA * EVEN_OUT ** -0.5),
        'od_w_in': nrm((N_ODD, D, ODD_IN), D ** -0.5),
        'od_cmp_pe_k': nrm((N_ODD, CMP_BLOCK, HEAD_DIM), 0.1),
        'od_cmp_pe_v': nrm((N_ODD, CMP_BLOCK, HEAD_DIM), 0.1),
        'od_cmp_w1_k': nrm((N_ODD, CMP_BLOCK * HEAD_DIM, CMP_HIDDEN), (CMP_BLOCK * HEAD_DIM) ** -0.5),
        'od_cmp_w2_k': nrm((N_ODD, CMP_HIDDEN, HEAD_DIM), CMP_HIDDEN ** -0.5),
        'od_cmp_w1_v': nrm((N_ODD, CMP_BLOCK * HEAD_DIM, CMP_HIDDEN), (CMP_BLOCK * HEAD_DIM) ** -0.5),
        'od_cmp_w2_v': nrm((N_ODD, CMP_HIDDEN, HEAD_DIM), CMP_HIDDEN ** -0.5),
        'od_pool_w': nrm((N_ODD, len(POOL_WINDOWS), POOL_GROUP_CH, POOL_GROUP_CH), POOL_GROUP_CH ** -0.5),
        'od_pool_scale': 1.0 + nrm((N_ODD, POOL_CH), 0.1),
        'od_w_out': nrm((N_ODD, ODD_OUT, D), DN_BETA * ODD_OUT ** -0.5),
        'rel_bias': nrm((REL_BUCKETS, NSA_HEADS), 0.5),
    }


def reference(x, c, ada_w, ada_b, ln_g, ln_b, ffn_w_gate, ffn_w_up, ffn_w_down,
              ev_w_in, ev_conv_a_w, ev_conv_a_b, ev_norm_a_g, ev_norm_a_b, ev_conv_b_w, ev_w_out,
              od_w_in, od_cmp_pe_k, od_cmp_pe_v, od_cmp_w1_k, od_cmp_w2_k, od_cmp_w1_v, od_cmp_w2_v,
              od_pool_w, od_pool_scale, od_w_out, rel_bias):
    bsz = x.shape[0]
    cond = jax.nn.silu(c)
    for layer in range(DEPTH):
        mod = (cond @ ada_w[layer] + ada_b[layer]).reshape(bsz, 3, 3, 1, D_MODEL)
        y = swiglu(adaln(x, mod[:, 0]), ffn_w_gate[layer, 0], ffn_w_up[layer, 0], ffn_w_down[layer, 0])
        x = deepnorm_update(x, y, mod[:, 0], FFN_RES_WEIGHT, ln_g[layer, 0], ln_b[layer, 0])
        h = adaln(x, mod[:, 1])
        j = layer // 2
        if layer % 2 == 0:
            y = even_mixer(h, ev_w_in[j], ev_conv_a_w[j], ev_conv_a_b[j], ev_norm_a_g[j],
                           ev_norm_a_b[j], ev_conv_b_w[j], ev_w_out[j])
        else:
            y = odd_mixer(h, od_w_in[j], od_cmp_pe_k[j], od_cmp_pe_v[j], od_cmp_w1_k[j], od_cmp_w2_k[j],
                          od_cmp_w1_v[j], od_cmp_w2_v[j], od_pool_w[j], od_pool_scale[j], od_w_out[j],
                          rel_bias)
        x = deepnorm_update(x, y, mod[:, 1], 1.0, ln_g[layer, 1], ln_b[layer, 1])
        y = swiglu(adaln(x, mod[:, 2]), ffn_w_gate[layer, 1], ffn_w_up[layer, 1], ffn_w_down[layer, 1])
        x = deepnorm_update(x, y, mod[:, 2], FFN_RES_WEIGHT, ln_g[layer, 2], ln_b[layer, 2])
    return x
```

```python
import math
from contextlib import ExitStack

import numpy as np
import concourse.bass as bass
import concourse.mybir as mybir
from concourse.bass_utils import run_bass_kernel_spmd

F32 = mybir.dt.float32
BF16 = mybir.dt.bfloat16
AF = mybir.ActivationFunctionType
ALU = mybir.AluOpType
AX = mybir.AxisListType

D = 1024
S = 2048
NT = S // 128
DFF = 2816
NJ = DFF // 128
DEPTH = 4
ALPHA = (2 * DEPTH) ** 0.25
EPS = 1e-5
NEGB = -30000.0
RW = (0.5, 1.0, 0.5)


class Ctx:
    def __init__(self, nc, es):
        self.nc = nc
        self.E = {"pe": nc.tensor, "act": nc.scalar, "dve": nc.vector, "pool": nc.gpsimd, "sp": nc.sync}
        self.sem, self.cnt = {}, {}
        self.seen = {e: {} for e in self.E}
        for e in ("pe", "act", "dve", "pool"):
            self.sem[e] = es.enter_context(nc.semaphore("s_" + e))
            self.cnt[e] = 0
        self.nslots = {"sp": 20, "pool": 20}
        self.rr = {"sp": 0, "pool": 0}
        for q, n in self.nslots.items():
            for j in range(n):
                k = ("d", q, j)
                self.sem[k] = es.enter_context(nc.semaphore("d_%s%d" % (q, j)))
                self.cnt[k] = 0
        self.res = {}

    def _deps(self, reads, writes):
        deps = {}
        for k in reads:
            st = self.res.get(k)
            if st and st[0]:
                i, c = st[0]
                if deps.get(i, 0) < c:
                    deps[i] = c
        for k in writes:
            st = self.res.get(k)
            if st:
                if st[0]:
                    i, c = st[0]
                    if deps.get(i, 0) < c:
                        deps[i] = c
                for i, c in st[1].items():
                    if deps.get(i, 0) < c:
                        deps[i] = c
        return deps

    def _wait(self, eng, deps):
        seen = self.seen[eng]
        for i, c in deps.items():
            if i == "pe" and eng == "pe":
                continue
            if seen.get(i, 0) < c:
                self.E[eng].wait_ge(self.sem[i], c)
                seen[i] = c

    def _mark(self, ident, c, reads, writes):
        for k in reads:
            st = self.res.get(k)
            if st is None:
                st = self.res[k] = [None, {}]
            st[1][ident] = c
        for k in writes:
            self.res[k] = [(ident, c), {}]

    def op(self, eng, fn, reads=(), writes=()):
        self._wait(eng, self._deps(reads, writes))
        ins = fn(self.E[eng])
        self.cnt[eng] += 1
        ins.then_inc(self.sem[eng], 1)
        self._mark(eng, self.cnt[eng], reads, writes)

    def dma(self, q, out, in_, reads=(), writes=(), **kw):
        j = self.rr[q]
        self.rr[q] = (j + 1) % self.nslots[q]
        ident = ("d", q, j)
        deps = self._deps(reads, writes)
        if self.cnt[ident]:
            deps[ident] = max(deps.get(ident, 0), self.cnt[ident])
        self._wait(q, deps)
        self.E[q].dma_start(out=out, in_=in_, **kw).then_inc(self.sem[ident], 16)
        self.cnt[ident] += 16
        self._mark(ident, self.cnt[ident], reads, writes)

    def barrier(self):
        tot = {k: c for k, c in self.cnt.items() if c}
        for e in self.E:
            self._wait(e, dict(tot))
        self.res = {}


def build(stop_after=None, nseq=2):
    nc = bass.Bass("TRN2", target_bir_lowering=False)
    es = ExitStack()

    def din(name, shape):
        return nc.dram_tensor(name, list(shape), F32, kind="ExternalInput").ap()

    x_d = din("x", (2, S, D))
    cT_d = din("cT", (D, 2))
    ada_w = din("ada_w", (DEPTH, D, 9 * D))
    ada_b = din("ada_b", (DEPTH, 9 * D))
    ln_g = din("ln_g", (DEPTH, 3, D))
    ln_b = din("ln_b", (DEPTH, 3, D))
    w_gate = din("ffn_w_gate", (DEPTH, 2, D, DFF))
    w_up = din("ffn_w_up", (DEPTH, 2, D, DFF))
    w_down = din("ffn_w_down", (DEPTH, 2, DFF, D))
    ev_w_in = din("ev_w_in", (2, D, 2560))
    ev_caw = din("ev_conv_a_wT", (2, 512, 31))
    ev_cab = din("ev_conv_a_b", (2, 512))
    ev_nag = din("ev_norm_a_g", (2, 512))
    ev_nab = din("ev_norm_a_b", (2, 512))
    ev_cbw = din("ev_conv_b_wT", (2, 512, 3))
    ev_w_out = din("ev_w_out", (2, 1024, D))
    ident_d = din("ident", (128, 128))
    od_w_in = din("od_w_in", (2, D, 1712))
    od_peT = din("od_peT", (2, 128, 32))
    od_w1k = din("od_cmp_w1_k", (2, 2048, 128))
    od_w1v = din("od_cmp_w1_v", (2, 2048, 128))
    od_w2k = din("od_cmp_w2_k", (2, 128, 64))
    od_w2v = din("od_cmp_w2_v", (2, 128, 64))
    od_pool_w = din("od_pool_w", (2, 4, 64, 64))
    od_pool_scale = din("od_pool_scale", (2, 256))
    od_w_out = din("od_w_out", (2, 1280, D))
    rbx_d = din("rbx", (33, 16))
    ohp_d = din("ohp", (33, 384))
    shiftbig_d = din("shiftbig", (17, 248))
    mlt4_d = din("mlt4", (128, 512))
    bi_d = din("bi", (32, 2048))
    abig_d = din("abig", (128, 64))
    bbig_d = din("bbig", (128, 64))
    ov_d = din("ov", (128, 32))
    corrw_d = din("corrw", (128, 2, 16))
    tabM_t = nc.dram_tensor("tabM", [16, 128, 384], F32, kind="Internal")
    tabM = tabM_t.ap()
    out_d = nc.dram_tensor("out", [2, S, D], F32, kind="ExternalOutput").ap()
    modscr = nc.dram_tensor("modscr", [DEPTH, 2, 9 * D], F32, kind="Internal").ap()

    cx = Ctx(nc, es)

    uid = [0]

    def sb(name, shape, dt, stack=None):
        uid[0] += 1
        return (stack or es).enter_context(nc.sbuf_tensor("%s_%d" % (name, uid[0]), list(shape), dt))

    def ps(name, shape, dt, stack=None):
        return (stack or es).enter_context(nc.psum_tensor(name, list(shape), dt))

    x_sb = sb("x_sb", (128, NT, D), F32)
    bcA1 = sb("bcA1", (128, D), F32)
    bcA0 = sb("bcA0", (128, D), F32)
    bcGt = sb("bcGt", (128, D), F32)
    bcLg = sb("bcLg", (128, D), F32)
    bcLb = sb("bcLb", (128, D), F32)
    ident_bf = sb("ident_bf", (128, 128), BF16)
    ident_f = sb("ident_f", (128, 128), F32)
    onesm = sb("onesm", (128, 128), F32)
    ps_t = [ps("ps_t%d" % i, (128, 1024), BF16) for i in range(1)]
    ps_a = [ps("ps_a%d" % i, (128, 512), F32) for i in range(3)]
    ps_o = [ps("ps_o%d" % i, (128, 512), F32) for i in range(2)]
    po_i = [0]
    ps_y = [ps("ps_y%d" % i, (128, 1024), F32) for i in range(1)]
    pa_i = [0]

    def next_pa():
        i = pa_i[0]
        pa_i[0] = (i + 1) % 3
        return ps_a[i], ("ps_a", i)

    cx.dma("pool", ident_bf[:], ident_d[:, :], writes=["ident_bf"])
    cx.dma("sp", ident_f[:], ident_d[:, :], writes=["ident_f"])
    cx.op("dve", lambda e: e.memset(onesm[:], 1.0 / 512.0), writes=["onesm"])
    epst = sb("epst", (128, 1), F32)
    cx.op("dve", lambda e: e.memset(epst[:], EPS), writes=["epst"])

    with ExitStack() as st:
        cT_sb = sb("cT_sb", (128, 8, 2), F32, st)
        condT = sb("condT", (128, 8, 2), BF16, st)
        modrow = sb("modrow", (2, 9 * D), F32, st)
        adab = sb("adab", (2, 9 * D), F32, st)
        wa = [sb("wa%d" % i, (128, 8, 512), BF16, st) for i in range(3)]
        cx.dma("sp", cT_sb[:], cT_d.rearrange("(k p) b -> p k b", p=128), writes=["cT_sb"])
        cx.op("act", lambda e: e.activation(out=condT[:], in_=cT_sb[:], func=AF.Silu),
              reads=["cT_sb"], writes=["condT"])
        for l in range(DEPTH):
            cx.dma("sp", adab[:], ada_b[l:l + 1, :].partition_broadcast(2) if False else
                   ada_b[l:l + 1, :].to_broadcast([2, 9 * D]), writes=["adab"])
            for n in range(18):
                wi = (l * 18 + n) % 3
                cx.dma("pool", wa[wi][:], ada_w[l, :, n * 512:(n + 1) * 512].rearrange("(k p) n -> p k n", p=128),
                       writes=[("wa", wi)])
                pt, pk = next_pa()
                for k in range(8):
                    cx.op("pe", lambda e, k=k, pt=pt, wi=wi: e.matmul(pt[0:2, :], condT[:, k, :], wa[wi][:, k, :],
                                                                       start=(k == 0), stop=(k == 7)),
                          reads=["condT", ("wa", wi)], writes=[pk])
                cx.op("dve", lambda e, pt=pt, n=n: e.tensor_tensor(out=modrow[:, n * 512:(n + 1) * 512], in0=pt[0:2, :],
                                                                 in1=adab[:, n * 512:(n + 1) * 512], op=ALU.add),
                      reads=[pk, "adab"], writes=["modrow"])
            for sub in range(3):
                o = sub * 3072
                cx.op("dve", lambda e, o=o: e.tensor_scalar_add(out=modrow[:, o + 1024:o + 2048],
                                                               in0=modrow[:, o + 1024:o + 2048], scalar1=1.0),
                      reads=["modrow"], writes=["modrow"])
                cx.op("dve", lambda e, o=o, sub=sub: e.tensor_scalar(out=modrow[:, o + 2048:o + 3072],
                                                                   in0=modrow[:, o + 2048:o + 3072],
                                                                   scalar1=1.0, scalar2=RW[sub], op0=ALU.add, op1=ALU.mult),
                      reads=["modrow"], writes=["modrow"])
            cx.dma("sp", modscr[l], modrow[:], reads=["modrow"], writes=[("modscr", l)])
        cx.barrier()

    def load_bc(l, sub, b):
        o = sub * 3072
        cx.dma("sp", bcA0[:], modscr[l, b:b + 1, o:o + 1024].to_broadcast([128, D]), reads=[("modscr", l)], writes=["bcA0"])
        cx.dma("sp", bcA1[:], modscr[l, b:b + 1, o + 1024:o + 2048].to_broadcast([128, D]), reads=[("modscr", l)], writes=["bcA1"])
        cx.dma("sp", bcGt[:], modscr[l, b:b + 1, o + 2048:o + 3072].to_broadcast([128, D]), reads=[("modscr", l)], writes=["bcGt"])
        cx.dma("sp", bcLg[:], ln_g[l, sub:sub + 1, :].to_broadcast([128, D]), writes=["bcLg"])
        cx.dma("sp", bcLb[:], ln_b[l, sub:sub + 1, :].to_broadcast([128, D]), writes=["bcLb"])

    def adaln_T(ti, hT, hT_key, col0, tmpf, hb, par, aux="pool", tk="tmpf", ew_only=False):
        cx.op("dve", lambda e: e.tensor_tensor(out=tmpf[:], in0=x_sb[:, ti, :], in1=bcA1[:], op=ALU.mult),
              reads=[("x", ti), "bcA1"], writes=[tk])
        cx.op(aux, lambda e: e.tensor_tensor(out=hb[par][:], in0=tmpf[:], in1=bcA0[:], op=ALU.add),
              reads=[tk, "bcA0"], writes=[("hb", par)])
        if ew_only:
            return
        adaln_Tb(hT, hT_key, col0, hb, par)

    def adaln_Tb(hT, hT_key, col0, hb, par):
        for k in range(8):
            cx.op("pe", lambda e, k=k: e.transpose(ps_t[0][:, k * 128:(k + 1) * 128], hb[par][:, k * 128:(k + 1) * 128], ident_bf[:]),
                  reads=[("hb", par), "ident_bf"], writes=["ps_t"])
        cx.op("act", lambda e: e.copy(out=hT[:, :, col0:col0 + 128], in_=ps_t[0][:].rearrange("p (k n) -> p k n", k=8)),
              reads=["ps_t"], writes=[hT_key])

    def deepnorm_a(ti, zt, stats, mv, ysrc=None):
        if ysrc is None:
            cx.op("dve", lambda e: e.tensor_tensor(out=zt[:], in0=ps_y[0][:], in1=bcGt[:], op=ALU.mult),
                  reads=["ps_y", "bcGt"], writes=["zt"])
        else:
            for hh, (yap, yk) in enumerate(ysrc):
                cx.op("dve", lambda e, hh=hh, yap=yap: e.tensor_tensor(out=zt[:, hh * 512:(hh + 1) * 512], in0=yap, in1=bcGt[:, hh * 512:(hh + 1) * 512], op=ALU.mult),
                      reads=[yk, "bcGt", "zt"], writes=["zt"])
        cx.op("dve", lambda e: e.scalar_tensor_tensor(out=zt[:], in0=x_sb[:, ti, :], scalar=ALPHA, in1=zt[:],
                                                       op0=ALU.mult, op1=ALU.add),
              reads=[("x", ti), "zt"], writes=["zt"])
        for hh in range(2):
            cx.op("dve", lambda e, hh=hh: e.bn_stats(out=stats[:, hh, :], in_=zt[:, hh * 512:(hh + 1) * 512]),
                  reads=["zt"], writes=[("stats", hh)])
        cx.op("dve", lambda e: e.bn_aggr(out=mv[:], in_=stats[:]), reads=[("stats", 0), ("stats", 1)], writes=["mv"])

    def deepnorm_b(mv, rstd):
        cx.op("act", lambda e: e.activation(out=rstd[:], in_=mv[:, 1:2], func=AF.Ln, bias=epst[:, 0:1], scale=1.0),
              reads=["mv", "epst"], writes=["rstd"])
        cx.op("act", lambda e: e.activation(out=rstd[:], in_=rstd[:], func=AF.Exp, scale=-0.5), reads=["rstd"], writes=["rstd"])

    def deepnorm_c(ti, zt, mv, rstd, nmr, aux="pool", dve_norm=False):
        if dve_norm:
            cx.op("dve", lambda e: e.tensor_scalar(out=zt[:], in0=zt[:], scalar1=mv[:, 0:1], scalar2=rstd[:, 0:1], op0=ALU.subtract, op1=ALU.mult),
                  reads=["zt", "mv", "rstd"], writes=["zt"])
        else:
            cx.op("dve", lambda e: e.scalar_tensor_tensor(out=nmr[:], in0=mv[:, 0:1], scalar=-1.0, in1=rstd[:],
                                                          op0=ALU.mult, op1=ALU.mult),
                  reads=["mv", "rstd"], writes=["nmr"])
            cx.op("act", lambda e: e.activation(out=zt[:], in_=zt[:], func=AF.Identity, bias=nmr[:, 0:1], scale=rstd[:, 0:1]),
                  reads=["zt", "rstd", "nmr"], writes=["zt"])
        cx.op(aux, lambda e: e.tensor_tensor(out=zt[:], in0=zt[:], in1=bcLg[:], op=ALU.mult),
              reads=["zt", "bcLg"], writes=["zt"])
        cx.op("dve", lambda e: e.tensor_tensor(out=x_sb[:, ti, :], in0=zt[:], in1=bcLb[:], op=ALU.add),
              reads=["zt", "bcLb"], writes=[("x", ti)])

    def deepnorm(ti, zt, stats, mv, rstd, nmr, ysrc=None, aux="pool"):
        deepnorm_a(ti, zt, stats, mv, ysrc)
        deepnorm_b(mv, rstd)
        deepnorm_c(ti, zt, mv, rstd, nmr, aux)

    def small_tiles(st):
        return (sb("zt", (128, D), F32, st), sb("stats", (128, 2, 6), F32, st), sb("mv", (128, 2), F32, st),
                sb("rstd", (128, 1), F32, st), sb("nmr", (128, 1), F32, st))

    def ffn(l, f, sub, b):
        with ExitStack() as st:
            hT = [sb("hT%d" % i, (128, 8, 512), BF16, st) for i in range(2)]
            actT = sb("actT", (128, NJ, 512), BF16, st)
            wd = sb("wd", (128, NJ, D), BF16, st)
            NWB = 3
            wg = [sb("wg%d" % i, (128, 8, 256), BF16, st) for i in range(NWB)]
            wu = [sb("wu%d" % i, (128, 8, 256), BF16, st) for i in range(NWB)]
            tmpf = sb("tmpf", (128, D), F32, st)
            hb = [sb("hb%d" % i, (128, D), BF16, st) for i in range(2)]
            sg = [sb("sg%d" % i, (128, 512), F32, st) for i in range(2)]
            zt, stats, mv, rstd, nmr = small_tiles(st)
            load_bc(l, sub, b)
            blocks = [(g, jb) for g in range(4) for jb in range(11)]

            def load_w(bi):
                g, jb = blocks[bi]
                wi = bi % NWB
                cx.dma("pool", wg[wi][:], w_gate[l, f, :, jb * 256:(jb + 1) * 256].rearrange("(k p) n -> p k n", p=128),
                       writes=[("wg", wi)])
                cx.dma("pool", wu[wi][:], w_up[l, f, :, jb * 256:(jb + 1) * 256].rearrange("(k p) n -> p k n", p=128),
                       writes=[("wu", wi)])

            load_w(0)
            load_w(1)

            def load_wd(c0):
                c1 = min(NJ, c0 + 6)
                cx.dma("pool", wd[:, c0:c1, :], w_down[l, f, c0 * 128:c1 * 128, :].rearrange("(c p) d -> p c d", p=128),
                       writes=[("wd", c0)])
            wdk = [("wd", c0) for c0 in range(0, NJ, 6)]
            for t in range(4):
                adaln_T(t, hT[0], ("hT", 0), t * 128, tmpf, hb, t % 2, aux="dve")
            deferred = []
            for g in range(4):
                hc = hT[g % 2]
                hk = ("hT", g % 2)
                for jb in range(11):
                    bi = g * 11 + jb
                    if bi + 2 < len(blocks):
                        load_w(bi + 2)
                    if g == 0 and 1 <= jb <= 4:
                        load_wd((jb - 1) * 6)
                    if g + 1 < 4 and 3 <= jb <= 6:
                        t = jb - 3
                        adaln_Tb(hT[(g + 1) % 2], ("hT", (g + 1) % 2), t * 128, hb, t % 2)
                    if g + 1 < 4 and 1 <= jb <= 4:
                        t = jb - 1
                        adaln_T((g + 1) * 4 + t, hT[(g + 1) % 2], ("hT", (g + 1) % 2), t * 128, tmpf, hb, t % 2, aux="dve", ew_only=True)
                    wi = bi % NWB
                    for jj in range(2):
                        j = jb * 2 + jj
                        pg, pgk = next_pa()
                        pu, puk = next_pa()
                        for k in range(8):
                            cx.op("pe", lambda e, k=k, pg=pg, wi=wi, jj=jj: e.matmul(pg[:], wg[wi][:, k, jj * 128:(jj + 1) * 128], hc[:, k, :],
                                                                                    start=(k == 0), stop=(k == 7)),
                                  reads=[("wg", wi), hk], writes=[pgk])
                        for k in range(8):
                            cx.op("pe", lambda e, k=k, pu=pu, wi=wi, jj=jj: e.matmul(pu[:], wu[wi][:, k, jj * 128:(jj + 1) * 128], hc[:, k, :],
                                                                                    start=(k == 0), stop=(k == 7)),
                                  reads=[("wu", wi), hk], writes=[puk])
                        si = j % 2
                        cx.op("act", lambda e, pg=pg, si=si: e.activation(out=sg[si][:], in_=pg[:], func=AF.Silu),
                              reads=[pgk], writes=[("sg", si)])
                        cx.op("dve", lambda e, pu=pu, si=si, j=j: e.tensor_tensor(out=actT[:, j, :], in0=sg[si][:], in1=pu[:], op=ALU.mult),
                              reads=[puk, ("sg", si)], writes=[("actT", j)])
                    if deferred and deferred[0][0] <= jb:
                        deferred.pop(0)[1]()
                for t in range(4):
                    if t % 2 == 0:
                        ysrc = [(ps_y[0][:, 0:512], ("yh", 0)), (ps_y[0][:, 512:1024], ("yh", 1))]
                    else:
                        ysrc = [(ps_o[0][:], ("ps_o", 0)), (ps_o[1][:], ("ps_o", 1))]
                    for nh in range(2):
                        yap, yk = ysrc[nh]
                        for j in range(NJ):
                            cx.op("pe", lambda e, t=t, nh=nh, j=j, yap=yap: e.matmul(yap, actT[:, j, t * 128:(t + 1) * 128],
                                                                                   wd[:, j, nh * 512:(nh + 1) * 512], start=(j == 0), stop=(j == NJ - 1)),
                                  reads=[("actT", j)] + (wdk if j == 0 else []), writes=[yk])
                    if t == 3 and g < 3:
                        deferred.append((1, lambda g=g, t=t, ysrc=ysrc: deepnorm_a(g * 4 + t, zt, stats, mv, ysrc)))
                        deferred.append((2, lambda: deepnorm_b(mv, rstd)))
                        deferred.append((3, lambda g=g, t=t: deepnorm_c(g * 4 + t, zt, mv, rstd, nmr, aux="dve", dve_norm=True)))
                    else:
                        deepnorm(g * 4 + t, zt, stats, mv, rstd, nmr, ysrc=ysrc, aux="dve")
            cx.barrier()

    def even(l, b):
        jl = l // 2
        with ExitStack() as st:
            hT = [sb("hT0", (128, 8, 512), BF16, st)] * 2
            hb = [sb("hb0", (128, D), BF16, st)] * 2
            U = sb("U", (128, 4, 30 + 512), BF16, st)
            M = sb("M", (128, 2 + 512), F32, st)
            Mh = sb("Mh", (128, 4, 2), F32, st)
            GB = [sb("GB0", (128, 512), F32, st)] * 2
            CA = sb("CA", (128, 4, 512), F32, st)
            SQ = [sb("SQ0", (128, 512), F32, st)] * 2
            CB = sb("CB", (128, 512), F32, st)
            uzT = sb("uzT", (128, 8, 512), BF16, st)
            w5 = [sb("w5_%d" % i, (128, 8, 5, 128), BF16, st) for i in range(2)]
            wo = sb("wo", (128, 8, D), BF16, st)
            dg = sb("dg", (128, 4, 31, 128), BF16, st)
            caw = sb("caw", (128, 4, 31), F32, st)
            cab = sb("cab", (128, 4), F32, st)
            nag = sb("nag", (128, 4), F32, st)
            nab = sb("nab", (128, 4), F32, st)
            cbw = sb("cbw", (128, 4, 3), F32, st)
            sgm = sb("sgm", (128, 512), F32, st)
            tcp = sgm
            mean_sb = sb("mean_sb", (128, 512), F32, st)
            rs_sb = sb("rs_sb", (128, 512), F32, st)
            dtile = CB
            zt, stats, mv, rstd, nmr = small_tiles(st)
            tmpf = zt
            load_bc(l, 1, b)
            cx.dma("sp", caw[:], ev_caw[jl].rearrange("(c p) k -> p c k", p=128), writes=["caw"])
            cx.dma("sp", cbw[:], ev_cbw[jl].rearrange("(c p) k -> p c k", p=128), writes=["cbw"])
            for (tl, src, nm) in ((cab, ev_cab, "cab"), (nag, ev_nag, "nag"), (nab, ev_nab, "nab")):
                cx.dma("sp", tl[:], src[jl].rearrange("(c p) -> p c", p=128), writes=[nm], allow_slow_non_contiguous=True)
            w_in5 = ev_w_in[jl].rearrange("(k p) (g c n) -> p k g c n", p=128, g=5, c=4)
            chunks = [(g, c) for g in range(4) for c in range(4)]

            def load_w5(ci):
                g, c = chunks[ci]
                wi = ci % 2
                for gg in range(5):
                    cx.dma("pool", w5[wi][:, :, gg, :], w_in5[:, :, gg, c, :], writes=[("w5", wi, gg)])

            load_w5(0)
            cx.dma("pool", wo[:], ev_w_out[jl].rearrange("(c p) d -> p c d", p=128), writes=["wo"])
            cx.op("dve", lambda e: e.memset(U[:, :, 0:30], 0.0), writes=["Uh"])
            cx.op("dve", lambda e: e.memset(Mh[:], 0.0), writes=["Mh"])
            for t in range(4):
                adaln_T(t, hT[0], ("hT", 0), t * 128, tmpf, hb, 0, aux="dve", tk="zt")
            deferred = []
            for g in range(4):
                hc, hk = hT[0], ("hT", 0)
                if deferred and deferred[0][0] < 0:
                    deferred.pop(0)[1]()
                for c in range(4):
                    ci = g * 4 + c
                    wi = ci % 2
                    if ci + 1 < len(chunks):
                        load_w5(ci + 1)
                    gbi = 0
                    if g == 0:
                        for k in range(31):
                            cx.op("dve", lambda e, c=c, k=k: e.tensor_scalar(out=dg[:, c, k, :], in0=ident_f[:], scalar1=caw[:, c, k:k + 1], scalar2=None, op0=ALU.mult),
                                  reads=["ident_f", "caw"], writes=[("dg", c)])
                    for gg in (1, 0, 3, 4, 2):
                        pt, pk = next_pa()
                        for k in range(8):
                            cx.op("pe", lambda e, k=k, pt=pt, wi=wi, gg=gg: e.matmul(pt[:], w5[wi][:, k, gg, :], hc[:, k, :],
                                                                                    start=(k == 0), stop=(k == 7)),
                                  reads=[("w5", wi, gg), hk], writes=[pk])
                        if gg == 1:
                            cx.op("act", lambda e, pt=pt: e.activation(out=sgm[:], in_=pt[:], func=AF.Sigmoid),
                                  reads=[pk], writes=["sgm"])
                        elif gg == 0:
                            cx.op("dve", lambda e, pt=pt, c=c: e.tensor_tensor(out=U[:, c, 30:542], in0=pt[:], in1=sgm[:], op=ALU.mult),
                                  reads=[pk, "sgm"], writes=[("U", c)])
                        elif gg == 3:
                            cx.op("act", lambda e, pt=pt: e.copy(out=tcp[:], in_=pt[:]), reads=[pk], writes=["sgm"])
                        elif gg == 4:
                            cx.op("dve", lambda e, pt=pt, c=c: e.tensor_tensor(out=M[:, 2:514], in0=pt[:], in1=tcp[:], op=ALU.mult),
                                  reads=[pk, "sgm"], writes=["M"])
                            cx.op("dve", lambda e, c=c: e.tensor_copy(out=M[:, 0:2], in_=Mh[:, c, :]), reads=["Mh"], writes=["M0"])
                        else:
                            cx.op("act", lambda e, pt=pt, gbi=gbi: e.copy(out=GB[gbi][:], in_=pt[:]), reads=[pk], writes=[("GB", gbi)])
                    pc, pck = ps_o[ci % 2], ("ps_o", ci % 2)
                    for k in range(31):
                        cx.op("pe", lambda e, c=c, k=k, pc=pc: e.matmul(pc[:], dg[:, c, k, :], U[:, c, k:k + 512], start=(k == 0), stop=(k == 30)),
                              reads=[("U", c), "Uh", ("dg", c)], writes=[pck])
                    cx.op("act", lambda e, c=c, pc=pc: e.activation(out=CA[:, c, :], in_=pc[:], func=AF.Identity, bias=cab[:, c:c + 1], scale=1.0),
                          reads=[pck, "cab"], writes=[("CA", c)])
                    cx.op("dve", lambda e, c=c: e.tensor_scalar(out=CB[:], in0=M[:, 0:512], scalar1=cbw[:, c, 0:1], scalar2=None, op0=ALU.mult),
                          reads=["M", "M0", "cbw"], writes=["CB"])
                    for k in range(1, 3):
                        cx.op("dve", lambda e, c=c, k=k: e.scalar_tensor_tensor(out=CB[:], in0=M[:, k:k + 512], scalar=cbw[:, c, k:k + 1],
                                                                               in1=CB[:], op0=ALU.mult, op1=ALU.add),
                              reads=["M", "M0", "CB"], writes=["CB"])
                    cx.op("dve", lambda e, c=c: e.tensor_copy(out=Mh[:, c, :], in_=M[:, 512:514]), reads=["M"], writes=["Mh"])
                    cx.op("dve", lambda e, c=c, gbi=gbi: e.tensor_tensor(out=uzT[:, 4 + c, :], in0=CB[:], in1=GB[gbi][:], op=ALU.mult),
                          reads=["CB", ("GB", gbi)], writes=[("uzT", 4 + c)])
                    if deferred and deferred[0][0] <= c:
                        deferred.pop(0)[1]()
                if g + 1 < 4:
                    for t in range(4):
                        adaln_T((g + 1) * 4 + t, hT[0], ("hT", 0), t * 128, tmpf, hb, 0, aux="dve", tk="zt")
                cak = [("CA", c) for c in range(4)]
                pm, pmk = next_pa()
                pq, pqk = next_pa()
                for c in range(4):
                    cx.op("pe", lambda e, c=c: e.matmul(pm[:], onesm[:], CA[:, c, :], start=(c == 0), stop=(c == 3)),
                          reads=cak + ["onesm"], writes=[pmk])
                for c in range(4):
                    cx.op("act", lambda e, c=c: e.activation(out=SQ[c % 2][:], in_=CA[:, c, :], func=AF.Square), reads=[("CA", c)], writes=[("SQ", 0)])
                    cx.op("pe", lambda e, c=c: e.matmul(pq[:], onesm[:], SQ[c % 2][:], start=(c == 0), stop=(c == 3)),
                          reads=[("SQ", 0), "onesm"], writes=[pqk])
                cx.op("act", lambda e: e.copy(out=mean_sb[:], in_=pm[:]), reads=[pmk], writes=["mean_sb"])
                cx.op("dve", lambda e: e.tensor_tensor(out=rs_sb[:], in0=mean_sb[:], in1=mean_sb[:], op=ALU.mult),
                      reads=["mean_sb"], writes=["rs_sb"])
                cx.op("dve", lambda e: e.tensor_tensor(out=rs_sb[:], in0=pq[:], in1=rs_sb[:], op=ALU.subtract),
                      reads=[pqk, "rs_sb"], writes=["rs_sb"])
                cx.op("act", lambda e: e.activation(out=rs_sb[:], in_=rs_sb[:], func=AF.Ln, bias=epst[:, 0:1], scale=1.0),
                      reads=["rs_sb", "epst"], writes=["rs_sb"])
                cx.op("act", lambda e: e.activation(out=rs_sb[:], in_=rs_sb[:], func=AF.Exp, scale=-0.5), reads=["rs_sb"], writes=["rs_sb"])
                for c in range(4):
                    cx.op("dve", lambda e, c=c: e.tensor_tensor(out=dtile[:], in0=CA[:, c, :], in1=mean_sb[:], op=ALU.subtract),
                          reads=[("CA", c), "mean_sb"], writes=["CB"])
                    cx.op("dve", lambda e: e.tensor_tensor(out=dtile[:], in0=dtile[:], in1=rs_sb[:], op=ALU.mult),
                          reads=["CB", "rs_sb"], writes=["CB"])
                    cx.op("act", lambda e, c=c: e.activation(out=uzT[:, c, :], in_=dtile[:], func=AF.Silu, bias=nab[:, c:c + 1], scale=nag[:, c:c + 1]),
                          reads=["CB", "nab", "nag"], writes=[("uzT", c)])
                cx.op("dve", lambda e: e.tensor_copy(out=U[:, :, 0:30], in_=U[:, :, 512:542]),
                      reads=[("U", c) for c in range(4)], writes=["Uh"])
                uzk = [("uzT", c) for c in range(8)]
                for t in range(4):
                    if t % 2 == 0:
                        ysrc = [(ps_y[0][:, 0:512], ("yh", 0)), (ps_y[0][:, 512:1024], ("yh", 1))]
                    else:
                        ysrc = [(ps_o[0][:], ("ps_o", 0)), (ps_o[1][:], ("ps_o", 1))]
                    for nh in range(2):
                        yap, yk = ysrc[nh]
                        for ci_, c in enumerate((4, 5, 6, 7, 0, 1, 2, 3)):
                            cx.op("pe", lambda e, t=t, nh=nh, c=c, ci_=ci_, yap=yap: e.matmul(yap, uzT[:, c, t * 128:(t + 1) * 128],
                                                                                            wo[:, c, nh * 512:(nh + 1) * 512], start=(ci_ == 0), stop=(ci_ == 7)),
                                  reads=[("uzT", c), "wo"], writes=[yk])
                    if t == 3 and g < 3:
                        deferred.append((-1, lambda g=g, t=t, ysrc=ysrc: deepnorm_a(g * 4 + t, zt, stats, mv, ysrc)))
                        deferred.append((0, lambda: deepnorm_b(mv, rstd)))
                        deferred.append((1, lambda g=g, t=t: deepnorm_c(g * 4 + t, zt, mv, rstd, nmr, aux="dve", dve_norm=True)))
                    else:
                        deepnorm(g * 4 + t, zt, stats, mv, rstd, nmr, ysrc=ysrc, aux="dve")
            cx.barrier()

    def build_tables():
        with ExitStack() as st:
            rbx_sb = sb("rbx_sb", (33, 16), F32, st)
            ohp_sb = sb("ohp_sb", (33, 384), F32, st)
            rbb = sb("rbb", (33, 16, 128), F32, st)
            tabS = [sb("tabS%d" % i, (128, 384), F32, st) for i in range(2)]
            cx.dma("sp", rbx_sb[:], rbx_d[:, :], writes=["rbx_sb"])
            cx.dma("sp", ohp_sb[:], ohp_d[:, :], writes=["ohp_sb"])
            cx.op("dve", lambda e: e.tensor_copy(out=rbb[:], in_=rbx_sb[:].unsqueeze(2).to_broadcast([33, 16, 128])),
                  reads=["rbx_sb"], writes=["rbb"])
            for h in range(16):
                pt, pk = next_pa()
                cx.op("pe", lambda e, h=h, pt=pt: e.matmul(pt[:, 0:384], rbb[:, h, :], ohp_sb[:], start=True, stop=True),
                      reads=["rbb", "ohp_sb"], writes=[pk])
                cx.op("act", lambda e, h=h, pt=pt: e.copy(out=tabS[h % 2][:], in_=pt[:, 0:384]), reads=[pk], writes=[("tabS", h % 2)])
                cx.dma("sp", tabM[h], tabS[h % 2][:], reads=[("tabS", h % 2)], writes=["tabM"])
            cx.barrier()

    def skew(off, pstride, nparts):
        return bass.AP(tabM_t, off, [[pstride, nparts], [2 * 128 * 384, 4], [1, 128]])

    def odd(l, b):
        jl = l // 2
        with ExitStack() as so:
            kvcT = sb("kvcT", (128, S), BF16, so)
            ksA = sb("ksA", (128, S), BF16, so)
            ksB = sb("ksB", (128, S), BF16, so)
            kwA = sb("kwA", (128, S), BF16, so)
            kwB = sb("kwB", (128, S), BF16, so)
            vsw = sb("vsw", (128, NT, 2, 65), BF16, so)
            ypT = sb("ypT", (128, 2, S), BF16, so)
            biasP = sb("biasP", (128, 4, 4, 128), BF16, so)
            biasD = sb("biasD", (128, 4, 4, 128), BF16, so)
            Bc = sb("Bc", (17, 4, 4, 128), BF16, so)
            shiftb = sb("shiftb", (17, 248), BF16, so)
            mlt4 = sb("mlt4", (128, 512), BF16, so)
            bi_sb = sb("bi_sb", (128, S), BF16, so)
            abig = sb("abig", (128, 64), F32, so)
            bbig = sb("bbig", (128, 64), F32, so)
            kcA = sb("kcA", (128, 128), BF16, so)
            kcB = sb("kcB", (128, 128), BF16, so)
            vcx = sb("vcx", (128, 97), BF16, so)
            zt, stats, mv, rstd, nmr = small_tiles(so)
            tmpf = zt
            hb = [sb("hb0", (128, D), BF16, so)] * 2
            load_bc(l, 1, b)
            win = od_w_in[jl]
            with ExitStack() as st:
                hT = sb("hT", (128, 8, 512), BF16, st)
                wA = sb("wA", (128, 8, 768), BF16, st)
                uP = sb("uP", (128, 2, 527), F32, st)
                s2 = sb("s2", (128, 2, 527), F32, st)
                s4 = sb("s4", (128, 2, 527), F32, st)
                ssel = sb("ssel", (128, 2, 512), F32, st)
                dP = sb("dP", (128, 2, 512), BF16, st)
                wbd = sb("wbd", (128, 2, 128), BF16, st)
                pscale = sb("pscale", (128, 2), F32, st)
                corrw = sb("corrw", (128, 2, 16), F32, st)
                w1 = sb("w1", (128, 32, 128), BF16, st)
                peT = sb("peT", (128, 32), BF16, st)
                w2k = sb("w2k", (128, 128), BF16, st)
                w2v = sb("w2v", (128, 64), BF16, st)
                c1 = sb("c1", (128, 2), F32, st)
                hid = sb("hid", (128, 2, 128), BF16, st)

                def wcols(dst0, src0, n):
                    cx.dma("pool", wA[:, :, dst0:dst0 + n], win[:, src0:src0 + n].rearrange("(k p) n -> p k n", p=128),
                           writes=[("wA", dst0)])
                wcols(0, 1024, 128)
                wcols(128, 1152, 64); wcols(192, 1152, 64)
                wcols(256, 1280, 64); wcols(320, 1280, 64)
                wcols(384, 1456, 256)
                wcols(640, 1216, 64); wcols(704, 1344, 64)
                wAk = [("wA", d0) for d0 in (0, 128, 192, 256, 320, 384, 640, 704)]
                for t in range(4):
                    adaln_T(t, hT, "hT", t * 128, tmpf, hb, 0, tk="zt")
                cx.op("dve", lambda e: e.memset(Bc[:], NEGB), writes=["Bc"])
                for hg in range(4):
                    h0 = 8 * (hg // 2) + (hg % 2)
                    cx.dma("pool", biasP[:, hg, :, :], skew(255 + h0 * 49152, 383, 128), reads=["tabM"], writes=["biasP"])
                    cx.dma("pool", biasD[:, hg, :, :], skew(127 + h0 * 49152, 383, 128), reads=["tabM"], writes=["biasD"])
                    cx.dma("pool", Bc[0:16, hg, :, :], skew(240 + h0 * 49152, 368, 16), reads=["tabM"], writes=["Bc"])
                for nm, tl in (("ksA", ksA), ("ksB", ksB), ("kwA", kwA), ("kwB", kwB), ("bi_sb", bi_sb), ("kcA", kcA), ("kcB", kcB)):
                    cx.op("dve", lambda e, tl=tl: e.memset(tl[:], 0.0), writes=[nm])
                cx.dma("pool", shiftb[:], shiftbig_d[:, :], writes=["shiftb"])
                cx.dma("pool", mlt4[:], mlt4_d[:, :], writes=["mlt4"])
                cx.dma("pool", bi_sb[0:32, :], bi_d[:, :], writes=["bi_sb"])
                cx.dma("sp", abig[:], abig_d[:, :], writes=["abig"])
                cx.dma("sp", bbig[:], bbig_d[:, :], writes=["bbig"])
                cx.op("dve", lambda e: e.memset(vsw[:, :, :, 64:65], 1.0), writes=["vsw1"])
                cx.op("dve", lambda e: e.memset(vcx[:], 0.0), writes=["vcx"])
                cx.op("dve", lambda e: e.memset(vcx[:, 64:65], 1.0), reads=[], writes=["vcx"])
                cx.dma("pool", vcx[:, 65:97], ov_d[:, :], writes=["vcx"])

                cx.op("dve", lambda e: e.memset(wbd[:], 0.0), writes=["wbd"])
                for gi in range(4):
                    hh, cc = gi % 2, gi // 2
                    cx.dma("pool", wbd[hh * 64:(hh + 1) * 64, cc, hh * 64:(hh + 1) * 64], od_pool_w[jl, gi], reads=[], writes=["wbd"])
                cx.dma("sp", pscale[:], od_pool_scale[jl].rearrange("(c p) -> p c", p=128), writes=["pscale"], allow_slow_non_contiguous=True)
                cx.dma("sp", corrw[:], corrw_d[:, :, :], writes=["corrw"])
                cx.dma("pool", w1[0:64, :, :], od_w1k[jl].rearrange("(l d) m -> d l m", d=64), writes=["w1k"])
                cx.dma("pool", w1[64:128, :, :], od_w1v[jl].rearrange("(l d) m -> d l m", d=64), writes=["w1v"])
                cx.dma("pool", peT[:], od_peT[jl], writes=["peT"])
                cx.dma("pool", w2k[:, 0:64], od_w2k[jl], writes=["w2k"])
                cx.dma("pool", w2k[:, 64:128], od_w2k[jl], writes=["w2k"])
                cx.dma("pool", w2v[:], od_w2v[jl], writes=["w2v"])
                cx.op("dve", lambda e: e.memset(uP[:, :, 0:15], 0.0), writes=["uPh"])
                cx.op("dve", lambda e: e.memset(hid[:], 0.0), writes=["hid"])
                for g in range(4):
                    if g > 0:
                        for t in range(4):
                            adaln_T(g * 4 + t, hT, "hT", t * 128, tmpf, hb, 0, tk="zt")
                    gs = slice(g * 512, (g + 1) * 512)
                    for ch in range(5):
                        pt, pk = next_pa()
                        for k in range(8):
                            cx.op("pe", lambda e, k=k, pt=pt, ch=ch: e.matmul(pt[:], wA[:, k, ch * 128:(ch + 1) * 128], hT[:, k, :],
                                                                            start=(k == 0), stop=(k == 7)),
                                  reads=wAk + ["hT"], writes=[pk])
                        if ch == 0:
                            cx.op("act", lambda e, pt=pt: e.copy(out=kvcT[:, gs], in_=pt[:]), reads=[pk], writes=[("kvcT", g)])
                        elif ch in (1, 2):
                            tA, tB, nm = (ksA, ksB, "ks") if ch == 1 else (kwA, kwB, "kw")
                            cx.op("act", lambda e, pt=pt, tA=tA: e.copy(out=tA[0:64, gs], in_=pt[0:64, :]), reads=[pk, nm + "A"], writes=[nm + "A"])
                            cx.op("dve", lambda e, pt=pt, tB=tB: e.tensor_copy(out=tB[64:128, gs], in_=pt[64:128, :]), reads=[pk, nm + "B"], writes=[nm + "B"])
                        else:
                            cx.op("act", lambda e, pt=pt, ch=ch: e.copy(out=uP[:, ch - 3, 15:527], in_=pt[:]), reads=[pk], writes=[("uP", ch - 3)])
                    for t in range(4):
                        ti = g * 4 + t
                        pt, pk = next_pa()
                        for k in range(8):
                            cx.op("pe", lambda e, k=k, pt=pt, t=t: e.matmul(pt[:, 0:128], hT[:, k, t * 128:(t + 1) * 128], wA[:, k, 640:768],
                                                                          start=(k == 0), stop=(k == 7)),
                                  reads=wAk + ["hT"], writes=[pk])
                        cx.op("act", lambda e, pt=pt, ti=ti: e.copy(out=vsw[:, ti, :, 0:64], in_=pt[:, 0:128].rearrange("p (a d) -> p a d", a=2)),
                              reads=[pk], writes=[("vsw", ti)])
                    uk = [("uP", 0), ("uP", 1), "uPh"]
                    cx.op("dve", lambda e: e.tensor_tensor(out=s2[:, :, 1:527], in0=uP[:, :, 1:527], in1=uP[:, :, 0:526], op=ALU.add),
                          reads=uk, writes=["s2"])
                    cx.op("dve", lambda e: e.tensor_scalar(out=ssel[0:64, 0, :], in0=s2[0:64, 0, 15:527], scalar1=0.5, scalar2=None, op0=ALU.mult),
                          reads=["s2"], writes=["ssel"])
                    cx.op("dve", lambda e: e.tensor_tensor(out=s4[:, :, 3:527], in0=s2[:, :, 3:527], in1=s2[:, :, 1:525], op=ALU.add),
                          reads=["s2"], writes=["s4"])
                    cx.op("dve", lambda e: e.tensor_scalar(out=ssel[64:128, 0, :], in0=s4[64:128, 0, 15:527], scalar1=0.25, scalar2=None, op0=ALU.mult),
                          reads=["s4"], writes=["ssel"])
                    cx.op("dve", lambda e: e.tensor_tensor(out=s2[:, :, 7:527], in0=s4[:, :, 7:527], in1=s4[:, :, 3:523], op=ALU.add),
                          reads=["s4", "ssel"], writes=["s2"])
                    cx.op("dve", lambda e: e.tensor_scalar(out=ssel[0:64, 1, :], in0=s2[0:64, 1, 15:527], scalar1=0.125, scalar2=None, op0=ALU.mult),
                          reads=["s2"], writes=["ssel"])
                    cx.op("dve", lambda e: e.tensor_tensor(out=s4[:, :, 15:527], in0=s2[:, :, 15:527], in1=s2[:, :, 7:519], op=ALU.add),
                          reads=["s2", "ssel"], writes=["s4"])
                    cx.op("dve", lambda e: e.tensor_scalar(out=ssel[64:128, 1, :], in0=s4[64:128, 1, 15:527], scalar1=0.0625, scalar2=None, op0=ALU.mult),
                          reads=["s4"], writes=["ssel"])
                    if g == 0:
                        cx.op("dve", lambda e: e.tensor_tensor(out=ssel[:, :, 0:16], in0=ssel[:, :, 0:16], in1=corrw[:], op=ALU.mult),
                              reads=["ssel", "corrw"], writes=["ssel"])
                    cx.op("dve", lambda e: e.tensor_tensor(out=dP[:], in0=ssel[:], in1=uP[:, :, 15:527], op=ALU.subtract),
                          reads=["ssel"] + uk, writes=["dP"])
                    for c in range(2):
                        pt, pk = next_pa()
                        cx.op("pe", lambda e, c=c, pt=pt: e.matmul(pt[:], wbd[:, c, :], dP[:, c, :], start=True, stop=True),
                              reads=["wbd", "dP"], writes=[pk])
                        cx.op("act", lambda e, c=c, pt=pt: e.activation(out=ypT[:, c, gs], in_=pt[:], func=AF.Copy, scale=pscale[:, c:c + 1]),
                              reads=[pk, "pscale"], writes=[("ypT", g)])
                    cx.op("pool", lambda e: e.tensor_copy(out=uP[:, :, 0:15], in_=uP[:, :, 512:527]), reads=uk + ["dP", "s2"], writes=["uPh"])
                kvk = [("kvcT", g) for g in range(4)]
                for br in range(2):
                    P0 = slice(br * 64, (br + 1) * 64)
                    ph, phk = next_pa()
                    pc, pck = next_pa()
                    for li in range(32):
                        cx.op("pe", lambda e, li=li: e.matmul(ph[:, 0:127], w1[P0, li, :], kvcT[P0, li:li + 2017:16], start=(li == 0), stop=(li == 31)),
                              reads=kvk + ["w1k", "w1v"], writes=[phk])
                    for li in range(32):
                        cx.op("pe", lambda e, li=li: e.matmul(pc[:, 0:1], w1[P0, li, :], peT[P0, li:li + 1], start=(li == 0), stop=(li == 31)),
                              reads=["peT", "w1k", "w1v"], writes=[pck])
                    cx.op("act", lambda e: e.copy(out=c1[:, br:br + 1], in_=pc[:, 0:1]), reads=[pck], writes=[("c1", br)])
                    cx.op("act", lambda e: e.activation(out=hid[:, br, 0:127], in_=ph[:, 0:127], func=AF.Silu, bias=c1[:, br:br + 1], scale=1.0),
                          reads=[phk, ("c1", br), "hid"], writes=[("hid", br)])
                    po, pok = next_pa()
                    if br == 0:
                        cx.op("pe", lambda e: e.matmul(po[:, 0:127], w2k[:], hid[:, 0, 0:127], start=True, stop=True),
                              reads=["w2k", ("hid", 0)], writes=[pok])
                        cx.op("act", lambda e: e.copy(out=kcA[0:64, 0:127], in_=po[0:64, 0:127]), reads=[pok, "kcA"], writes=["kcA"])
                        cx.op("act", lambda e: e.copy(out=kcB[64:128, 0:127], in_=po[64:128, 0:127]), reads=[pok, "kcB"], writes=["kcB"])
                    else:
                        cx.op("pe", lambda e: e.matmul(po[0:127, 0:64], hid[:, 1, 0:127], w2v[:], start=True, stop=True),
                              reads=["w2v", ("hid", 1)], writes=[pok])
                        cx.op("act", lambda e: e.copy(out=vcx[0:127, 0:64], in_=po[0:127, 0:64]), reads=[pok, "vcx"], writes=["vcx"])
                cx.barrier()
            with ExitStack() as st:
                wq = sb("wq", (128, 8, 1024), BF16, st)
                wgt = sb("wgt", (128, 8, 48), BF16, st)
                wo = sb("wo_o", (128, 10, D), BF16, st)
                hTq = [sb("hTq%d" % i, (128, 8, 128), BF16, st) for i in range(2)]
                qT = [sb("qT%d" % i, (128, 8, 128), BF16, st) for i in range(2)]
                gsig = [sb("gsig%d" % i, (128, 48), F32, st) for i in range(2)]
                PT = [sb("PT%d" % i, (128, 512), BF16, st) for i in range(3)]
                acc_o = sb("acc_o", (128, D), F32, st)
                tmpo = [sb("tmpo0", (128, 4, 64), F32, st)] * 2
                tmpi = sb("tmpi", (128, 4, 32), F32, st)
                rden = [sb("rden%d" % i, (128, 4), F32, st) for i in range(2)]
                fac = [sb("fac%d" % i, (128, 4), F32, st) for i in range(2)]
                imph = sb("imph", (128, 4, 32), F32, st)
                impp = sb("impp", (128, 32), F32, st)
                cmp3 = sb("cmp3", (128, 32, 32), BF16, st)
                rank = sb("rank", (128, 32), F32, st)
                selT = sb("selT", (128, 4, 128), BF16, st)
                ob = sb("ob", (128, D), BF16, st)
                ocT = sb("ocT", (128, 8, 128), BF16, st)
                tmpfC = sb("tmpfC", (128, D), F32, st)
                cx.dma("pool", wq[:], win[:, 0:1024].rearrange("(k p) n -> p k n", p=128), writes=["wq"])
                cx.dma("pool", wgt[:], win[:, 1408:1456].rearrange("(k p) n -> p k n", p=128), writes=["wgt"])
                cx.dma("pool", wo[:], od_w_out[jl].rearrange("(c p) d -> p c d", p=128), writes=["wo"])
                acc_v = acc_o[:].rearrange("p (c r d) -> p c r d", r=2, d=64)
                cx.op("dve", lambda e: e.memset(selT[:], 0.0), writes=["selT"])
                ptc = [0]
                uc = [0]

                def prep_a(qb):
                    cur = qb % 2
                    adaln_T(qb, hTq[cur], ("hTq", cur), 0, tmpfC, hb, 0, tk="tmpfC", ew_only=True)

                def prep(qb):
                    cur = qb % 2
                    adaln_Tb(hTq[cur], ("hTq", cur), 0, hb, 0)
                    for half in range(2):
                        pt, pk = ps_y[0][:, half * 512:(half + 1) * 512], ("yh", half)
                        for c4 in range(4):
                            c = half * 4 + c4
                            for k in range(8):
                                cx.op("pe", lambda e, k=k, c=c, c4=c4, pt=pt: e.matmul(pt[:, c4 * 128:(c4 + 1) * 128], wq[:, k, c * 128:(c + 1) * 128], hTq[cur][:, k, :],
                                                                                     start=(k == 0), stop=(k == 7)),
                                      reads=["wq", ("hTq", cur)], writes=[pk])
                        cx.op("dve", lambda e, pt=pt, half=half: e.tensor_scalar(out=qT[cur][:, half * 4:half * 4 + 4, :], in0=pt.rearrange("p (c n) -> p c n", c=4),
                                                                               scalar1=0.125, scalar2=None, op0=ALU.mult),
                              reads=[pk], writes=[("qT", cur)])
                    pt, pk = ps_y[0][:, 0:512], ("yh", 0)
                    for k in range(8):
                        cx.op("pe", lambda e, k=k, pt=pt: e.matmul(pt[:, 0:48], hTq[cur][:, k, :], wgt[:, k, :], start=(k == 0), stop=(k == 7)),
                              reads=["wgt", ("hTq", cur)], writes=[pk])
                    cx.op("act", lambda e, pt=pt: e.activation(out=gsig[cur][:], in_=pt[:, 0:48], func=AF.Exp, scale=-1.0), reads=[pk], writes=[("gsig", cur)])
                    cx.op("dve", lambda e: e.tensor_scalar_add(out=gsig[cur][:], in0=gsig[cur][:], scalar1=1.0), reads=[("gsig", cur)], writes=[("gsig", cur)])
                    cx.op("dve", lambda e: e.reciprocal(out=gsig[cur][:], in_=gsig[cur][:]), reads=[("gsig", cur)], writes=[("gsig", cur)])

                def topk_dve(qb):
                    cx.op("dve", lambda e: e.tensor_reduce(out=impp[:], in_=imph[:].rearrange("p g s -> p s g"), axis=AX.X, op=ALU.add),
                          reads=[("imph", g) for g in range(4)], writes=["impp"])
                    cx.op("dve", lambda e: e.tensor_tensor(out=impp[:], in0=impp[:], in1=abig[:, 32 - 2 * qb:64 - 2 * qb], op=ALU.mult),
                          reads=["impp", "abig"], writes=["impp"])
                    cx.op("dve", lambda e: e.tensor_tensor(out=impp[:], in0=impp[:], in1=bbig[:, 32 - 2 * qb:64 - 2 * qb], op=ALU.add),
                          reads=["impp", "bbig"], writes=["impp"])
                    cx.op("dve", lambda e: e.memset(impp[:, 0:1], 3e30), reads=["impp"], writes=["impp"])
                    cx.op("dve", lambda e: e.tensor_tensor(out=cmp3[:], in0=impp[:].unsqueeze(1).to_broadcast([128, 32, 32]),
                                                           in1=impp[:].unsqueeze(2).to_broadcast([128, 32, 32]), op=ALU.is_gt),
                          reads=["impp"], writes=["cmp3"])
                    cx.op("dve", lambda e: e.tensor_reduce(out=rank[:], in_=cmp3[:], axis=AX.X, op=ALU.add), reads=["cmp3"], writes=["rank"])
                    cx.op("dve", lambda e: e.tensor_scalar(out=rank[:], in0=rank[:], scalar1=15.5, scalar2=-1.0, op0=ALU.is_gt, op1=ALU.mult),
                          reads=["rank"], writes=["rank"])

                def topk_pe():
                    pt, pk = ps_y[0][:, 512:1024], ("yh", 1)
                    cx.op("pe", lambda e, pt=pt: e.transpose(pt[0:32, 0:128], rank[:], ident_f[:]), reads=["rank", "ident_f"], writes=[pk])
                    cx.op("act", lambda e, pt=pt: e.copy(out=selT[0:32, :, :], in_=pt[0:32, 0:128].unsqueeze(1).to_broadcast([32, 4, 128])),
                          reads=[pk, "selT"], writes=["selT"])

                def attention(qb):
                    cur = qb % 2
                    gs_v = gsig[cur][:].rearrange("p (c r b) -> p c r b", r=2, b=3)
                    units = []
                    for br in (0, 2, 1):
                        for hg in range(4):
                            if br == 0:
                                kbs = [0]
                            elif br == 1:
                                kbs = list(range(0, qb + 1))
                            else:
                                kbs = list(range(max(0, qb - 4), qb + 1))
                            units.append((br, hg, kbs))
                    steps = [(u, ii) for u, (br, hg, kbs) in enumerate(units) for ii in range(len(kbs))]
                    pts = {}
                    pos = {}

                    def emit_S(si):
                        u, ii = steps[si]
                        br, hg, kbs = units[u]
                        kb = kbs[ii]
                        if br == 1 and hg == 0 and ii == 0 and qb >= 8:
                            topk_pe()
                        par, half = hg % 2, hg // 2
                        P0 = slice(par * 64, (par + 1) * 64)
                        hs = slice(4 * half, 4 * half + 4)
                        rhs_q = qT[cur][:, hs, :]
                        ex = []
                        if br == 0:
                            lhs = (kcA if par == 0 else kcB)[:, :]
                            ex.append((shiftb[:, 120 - 8 * qb:248 - 8 * qb], Bc[:, hg, :, :], ["shiftb", "Bc"]))
                        else:
                            kT_t = ((ksA, ksB) if br == 1 else (kwA, kwB))[par]
                            lhs = kT_t[:, kb * 128:(kb + 1) * 128]
                            if kb == qb:
                                ex.append((ident_bf[:], biasD[:, hg, :, :], ["ident_bf", "biasD"]))
                            if kb == qb - 1:
                                ex.append((ident_bf[:], biasP[:, hg, :, :], ["ident_bf", "biasP"]))
                            if br == 2 and kb == qb - 4:
                                ex.append((ident_bf[:], mlt4[:], ["ident_bf", "mlt4"]))
                            if br == 1 and qb >= 8:
                                ex.append((bi_sb[:, kb * 128:(kb + 1) * 128], selT[:].rearrange("p a t -> p (a t)"), ["bi_sb", "selT"]))
                        pt, pk = next_pa()
                        pts[si] = (pt, pk)
                        cx.op("pe", lambda e: e.matmul(pt[:], lhs, rhs_q, start=True, stop=(len(ex) == 0)),
                              reads=[("qT", cur), "kcA", "kcB"], writes=[pk])
                        for xi, (lh, rh, rk) in enumerate(ex):
                            cx.op("pe", lambda e, lh=lh, rh=rh, xi=xi: e.matmul(pt[:], lh, rh, start=False, stop=(xi == len(ex) - 1)),
                                  reads=rk, writes=[pk])

                    LOOK = 2
                    for si in range(min(LOOK, len(steps))):
                        emit_S(si)
                    for si, (u, ii) in enumerate(steps):
                        br, hg, kbs = units[u]
                        kb = kbs[ii]
                        par, half = hg % 2, hg // 2
                        hs = slice(4 * half, 4 * half + 4)
                        if si + LOOK < len(steps):
                            emit_S(si + LOOK)
                        if ii == 0:
                            oi = uc[0] % 2
                            uc[0] += 1
                            pos[u] = oi
                        oi = pos[u]
                        po, pok = ps_o[oi], ("ps_o", oi)
                        pov = po[:].rearrange("p (i c) -> p i c", i=4)
                        pt, pk = pts.pop(si)
                        pi = ptc[0] % 3
                        ptc[0] += 1
                        cx.op("act", lambda e, pt=pt, pi=pi: e.activation(out=PT[pi][:], in_=pt[:], func=AF.Exp),
                              reads=[pk], writes=[("PT", pi)])
                        nv = 97 if br == 0 else 65
                        vap = vcx[:, 0:97] if br == 0 else vsw[:, kb, br - 1, :]
                        for i in range(4):
                            cx.op("pe", lambda e, i=i, pi=pi, vap=vap, nv=nv, pov=pov, ii=ii, kbs=kbs: e.matmul(
                                pov[:, i, 0:nv], PT[pi][:, i * 128:(i + 1) * 128], vap,
                                start=(ii == 0 and i == 0), stop=(ii == len(kbs) - 1 and i == 3)),
                                  reads=[("PT", pi), "vcx"], writes=[pok])
                        if ii == len(kbs) - 1:
                            rd, fc, to = rden[oi], fac[oi], tmpo[oi]
                            if br == 0:
                                cx.op("dve", lambda e, rd=rd, pov=pov: e.tensor_scalar_add(out=rd[:], in0=pov[:, :, 64], scalar1=1e-30), reads=[pok], writes=[("rden", oi)])
                                cx.op("dve", lambda e, rd=rd: e.reciprocal(out=rd[:], in_=rd[:]), reads=[("rden", oi)], writes=[("rden", oi)])
                            else:
                                cx.op("dve", lambda e, rd=rd, pov=pov: e.reciprocal(out=rd[:], in_=pov[:, :, 64]), reads=[pok], writes=[("rden", oi)])
                            if br == 0 and qb >= 8:
                                cx.op("dve", lambda e, rd=rd, pov=pov: e.tensor_tensor(out=tmpi[:], in0=pov[:, :, 65:97], in1=rd[:].unsqueeze(2).to_broadcast([128, 4, 32]), op=ALU.mult),
                                      reads=[pok, ("rden", oi)], writes=["tmpi"])
                                cx.op("dve", lambda e, hg=hg: e.tensor_reduce(out=imph[:, hg, :], in_=tmpi[:].rearrange("p h s -> p s h"), axis=AX.X, op=ALU.add),
                                      reads=["tmpi"], writes=[("imph", hg)])
                            cx.op("dve", lambda e, rd=rd, fc=fc, hs=hs, par=par, br=br: e.tensor_tensor(out=fc[:], in0=rd[:], in1=gs_v[:, hs, par, br], op=ALU.mult),
                                  reads=[("rden", oi), ("gsig", cur)], writes=[("fac", oi)])
                            av = acc_v[:, hs, par, :]
                            facb = fc[:].unsqueeze(2).to_broadcast([128, 4, 64])
                            if br == 0:
                                cx.op("dve", lambda e, av=av, pov=pov, facb=facb: e.tensor_tensor(out=av, in0=pov[:, :, 0:64], in1=facb, op=ALU.mult),
                                      reads=[pok, ("fac", oi)], writes=[("acc", hg)])
                            else:
                                cx.op("dve", lambda e, to=to, pov=pov, facb=facb: e.tensor_tensor(out=to[:], in0=pov[:, :, 0:64], in1=facb, op=ALU.mult),
                                      reads=[pok, ("fac", oi)], writes=[("tmpo", 0)])
                                cx.op("pool", lambda e, av=av, to=to: e.tensor_tensor(out=av, in0=av, in1=to[:], op=ALU.add),
                                      reads=[("tmpo", 0), ("acc", hg)], writes=[("acc", hg)])
                            if u == 3 and qb >= 8:
                                topk_dve(qb)
                            while pending and pending[0][0] <= u:
                                pending.pop(0)[1]()
                            if u == 5 and qb + 1 < NT:
                                prep_a(qb + 1)
                            if u == 7 and qb + 1 < NT:
                                prep(qb + 1)
                    cx.op("pool", lambda e: e.tensor_copy(out=ob[:], in_=acc_o[:]), reads=[("acc", g) for g in range(4)], writes=["ob"])

                pending = []

                def tail_stages(qb):
                    ti = qb
                    ysrc = [(ps_y[0][:, 0:512], ("yh", 0)), (ps_y[0][:, 512:1024], ("yh", 1))]

                    def T1():
                        for k in range(8):
                            cx.op("pe", lambda e, k=k: e.transpose(ps_t[0][:, k * 128:(k + 1) * 128], ob[:, k * 128:(k + 1) * 128], ident_bf[:]),
                                  reads=["ob", "ident_bf"], writes=["ps_t"])
                        cx.op("act", lambda e: e.copy(out=ocT[:], in_=ps_t[0][:].rearrange("p (k n) -> p k n", k=8)), reads=["ps_t"], writes=["ocT"])

                    def T2():
                        for nh in range(2):
                            yap, yk = ysrc[nh]
                            for c in range(10):
                                lh = ocT[:, c, :] if c < 8 else ypT[:, c - 8, ti * 128:(ti + 1) * 128]
                                cx.op("pe", lambda e, nh=nh, c=c, lh=lh, yap=yap: e.matmul(yap, lh, wo[:, c, nh * 512:(nh + 1) * 512],
                                                                                        start=(c == 0), stop=(c == 9)),
                                      reads=["ocT", "wo"], writes=[yk])

                    def T3a():
                        deepnorm_a(ti, zt, stats, mv, ysrc)

                    def T3b():
                        deepnorm_b(mv, rstd)

                    def T3c():
                        deepnorm_c(ti, zt, mv, rstd, nmr, aux="pool", dve_norm=True)
                    return [(3, T1), (4, T2), (6, T3a), (8, T3b), (9, T3c)]

                prep_a(0)
                prep(0)
                for qb in range(NT):
                    attention(qb)
                    while pending:
                        pending.pop(0)[1]()
                    pending.extend(tail_stages(qb))
                while pending:
                    pending.pop(0)[1]()
                cx.barrier()

    build_tables()
    done = False
    for b in range(nseq):
        for t0 in range(0, NT, 4):
            cx.dma("sp", x_sb[:, t0:t0 + 4, :], x_d[b, t0 * 128:(t0 + 4) * 128, :].rearrange("(t p) d -> p t d", p=128),
                   writes=[("x", t) for t in range(t0, t0 + 4)])
        for l in range(DEPTH):
            for sub in range(3):
                if sub == 0:
                    ffn(l, 0, 0, b)
                elif sub == 2:
                    ffn(l, 1, 2, b)
                elif l % 2 == 0:
                    even(l, b)
                else:
                    odd(l, b)
                if stop_after == (l, sub):
                    done = True
                    break
            if done:
                break
        for t0 in range(0, NT, 4):
            cx.dma("sp", out_d[b, t0 * 128:(t0 + 4) * 128, :].rearrange("(t p) d -> p t d", p=128), x_sb[:, t0:t0 + 4, :],
                   reads=[("x", t) for t in range(t0, t0 + 4)], writes=[("out", b, t0)])
        cx.barrier()
    es.close()
    return nc


def t5_bucket_np(dist):
    n = np.maximum(dist, 0)
    nf = np.maximum(n, 1).astype(np.float32)
    large = 16 + (np.log(nf / np.float32(16)) / np.float32(math.log(128 / 16)) * np.float32(16)).astype(np.int32)
    return np.where(n < 16, n, np.minimum(large, 31))


def struct_consts():
    c = {}
    ohp = np.zeros((33, 384), np.float32)
    for i in range(384):
        dist = i - 127
        if dist < 0:
            ohp[32, i] = 1.0
        else:
            ohp[int(t5_bucket_np(np.array(dist))), i] += 1.0
            ohp[31, i] -= 1.0
    c["ohp"] = ohp
    sh = np.zeros((17, 248), np.float32)
    for r in range(16):
        sh[r, r + 111] = 1.0
    sh[16, 127:] = 1.0
    c["shiftbig"] = sh
    jl = np.arange(128)[:, None]
    tl = np.arange(128)[None, :]
    mlt = np.where(jl > tl, 0.0, NEGB).astype(np.float32)
    c["mlt4"] = np.ascontiguousarray(np.tile(mlt, (1, 4)))
    bi = np.zeros((32, 2048), np.float32)
    for s_ in range(32):
        bi[s_, s_ * 64:(s_ + 1) * 64] = 32768.0
    c["bi"] = bi
    ab = np.zeros((128, 64), np.float32)
    bb = np.zeros((128, 64), np.float32)
    for t in range(128):
        cur = 1 if t >= 64 else 0
        for sp in range(64):
            s2 = sp - 32
            if s2 == cur:
                bb[t, sp] = 2e30
            elif s2 == cur - 1:
                bb[t, sp] = 1e30
            elif s2 > cur:
                bb[t, sp] = -1e30
            else:
                ab[t, sp] = 1.0
    c["abig"], c["bbig"] = ab, bb
    ov = np.zeros((128, 32), np.float32)
    for n in range(127):
        st_, en = 16 * n, 16 * n + 32
        for s_ in range(32):
            o = min(en, 64 * s_ + 64) - max(st_, 64 * s_)
            if o > 0:
                ov[n, s_] = o / 32.0
    c["ov"] = ov
    cw = np.ones((128, 2, 16), np.float32)
    for p in range(128):
        for ch in range(2):
            w = (2, 4, 8, 16)[ch * 2 + p // 64]
            for t in range(16):
                cw[p, ch, t] = w / min(t + 1, w)
    c["corrw"] = cw
    return c


def make_in_maps(inputs, ncores=8):
    f = lambda a: np.ascontiguousarray(np.asarray(a, dtype=np.float32))
    shared = {k: f(inputs[k]) for k in ("ada_w", "ada_b", "ln_g", "ln_b", "ffn_w_gate", "ffn_w_up", "ffn_w_down",
                                        "ev_w_in", "ev_conv_a_b", "ev_norm_a_g", "ev_norm_a_b", "ev_w_out")}
    shared["ev_conv_a_wT"] = f(np.transpose(inputs["ev_conv_a_w"], (0, 2, 1)))
    shared["ev_conv_b_wT"] = f(np.transpose(inputs["ev_conv_b_w"], (0, 2, 1)))
    shared["ident"] = np.eye(128, dtype=np.float32)
    for k in ("od_w_in", "od_cmp_w1_k", "od_cmp_w1_v", "od_cmp_w2_k", "od_cmp_w2_v", "od_pool_w", "od_pool_scale", "od_w_out"):
        shared[k] = f(inputs[k])
    shared["od_peT"] = f(np.concatenate([np.transpose(inputs["od_cmp_pe_k"], (0, 2, 1)), np.transpose(inputs["od_cmp_pe_v"], (0, 2, 1))], axis=1))
    shared["rbx"] = f(np.concatenate([np.asarray(inputs["rel_bias"], np.float32), np.full((1, 16), NEGB, np.float32)], axis=0))
    shared.update(struct_consts())
    x = f(inputs["x"])
    c = f(inputs["c"])
    maps = []
    for i in range(ncores):
        m = dict(shared)
        m["x"] = x[2 * i:2 * i + 2]
        m["cT"] = f(c[2 * i:2 * i + 2].T)
        maps.append(m)
    return maps


def kernel(**inputs):
    nc = build()
    maps = make_in_maps(inputs)
    res = run_bass_kernel_spmd(nc, maps, core_ids=list(range(8)))
    return np.concatenate([r["out"] for r in res.results], axis=0)
```

```python
import math
from contextlib import ExitStack

import numpy as np
import concourse.bass as bass
import concourse.mybir as mybir
from concourse.bass_utils import run_bass_kernel_spmd

F32 = mybir.dt.float32
BF16 = mybir.dt.bfloat16
AF = mybir.ActivationFunctionType
ALU = mybir.AluOpType
AX = mybir.AxisListType

D = 1024
S = 2048
NT = S // 128
DFF = 2816
NJ = DFF // 128
DEPTH = 4
ALPHA = (2 * DEPTH) ** 0.25
EPS = 1e-5
NEGB = -30000.0
RW = (0.5, 1.0, 0.5)


class Ctx:
    def __init__(self, nc, es):
        self.nc = nc
        self.E = {"pe": nc.tensor, "act": nc.scalar, "dve": nc.vector, "pool": nc.gpsimd, "sp": nc.sync}
        self.sem, self.cnt = {}, {}
        self.seen = {e: {} for e in self.E}
        for e in ("pe", "act", "dve", "pool"):
            self.sem[e] = es.enter_context(nc.semaphore("s_" + e))
            self.cnt[e] = 0
        self.nslots = {"sp": 20, "pool": 20}
        self.rr = {"sp": 0, "pool": 0}
        for q, n in self.nslots.items():
            for j in range(n):
                k = ("d", q, j)
                self.sem[k] = es.enter_context(nc.semaphore("d_%s%d" % (q, j)))
                self.cnt[k] = 0
        self.res = {}

    def _deps(self, reads, writes):
        deps = {}
        for k in reads:
            st = self.res.get(k)
            if st and st[0]:
                i, c = st[0]
                if deps.get(i, 0) < c:
                    deps[i] = c
        for k in writes:
            st = self.res.get(k)
            if st:
                if st[0]:
                    i, c = st[0]
                    if deps.get(i, 0) < c:
                        deps[i] = c
                for i, c in st[1].items():
                    if deps.get(i, 0) < c:
                        deps[i] = c
        return deps

    def _wait(self, eng, deps):
        seen = self.seen[eng]
        for i, c in deps.items():
            if i == "pe" and eng == "pe":
                continue
            if seen.get(i, 0) < c:
                self.E[eng].wait_ge(self.sem[i], c)
                seen[i] = c

    def _mark(self, ident, c, reads, writes):
        for k in reads:
            st = self.res.get(k)
            if st is None:
                st = self.res[k] = [None, {}]
            st[1][ident] = c
        for k in writes:
            self.res[k] = [(ident, c), {}]

    def op(self, eng, fn, reads=(), writes=()):
        self._wait(eng, self._deps(reads, writes))
        ins = fn(self.E[eng])
        self.cnt[eng] += 1
        ins.then_inc(self.sem[eng], 1)
        self._mark(eng, self.cnt[eng], reads, writes)

    def dma(self, q, out, in_, reads=(), writes=(), **kw):
        j = self.rr[q]
        self.rr[q] = (j + 1) % self.nslots[q]
        ident = ("d", q, j)
        deps = self._deps(reads, writes)
        if self.cnt[ident]:
            deps[ident] = max(deps.get(ident, 0), self.cnt[ident])
        self._wait(q, deps)
        self.E[q].dma_start(out=out, in_=in_, **kw).then_inc(self.sem[ident], 16)
        self.cnt[ident] += 16
        self._mark(ident, self.cnt[ident], reads, writes)

    def barrier(self):
        tot = {k: c for k, c in self.cnt.items() if c}
        for e in self.E:
            self._wait(e, dict(tot))
        self.res = {}


def build(stop_after=None, nseq=2):
    nc = bass.Bass("TRN2", target_bir_lowering=False)
    es = ExitStack()

    def din(name, shape):
        return nc.dram_tensor(name, list(shape), F32, kind="ExternalInput").ap()

    x_d = din("x", (2, S, D))
    cT_d = din("cT", (D, 2))
    ada_w = din("ada_w", (DEPTH, D, 9 * D))
    ada_b = din("ada_b", (DEPTH, 9 * D))
    ln_g = din("ln_g", (DEPTH, 3, D))
    ln_b = din("ln_b", (DEPTH, 3, D))
    w_gate = din("ffn_w_gate", (DEPTH, 2, D, DFF))
    w_up = din("ffn_w_up", (DEPTH, 2, D, DFF))
    w_down = din("ffn_w_down", (DEPTH, 2, DFF, D))
    ev_w_in = din("ev_w_in", (2, D, 2560))
    ev_caw = din("ev_conv_a_wT", (2, 512, 31))
    ev_cab = din("ev_conv_a_b", (2, 512))
    ev_nag = din("ev_norm_a_g", (2, 512))
    ev_nab = din("ev_norm_a_b", (2, 512))
    ev_cbw = din("ev_conv_b_wT", (2, 512, 3))
    ev_w_out = din("ev_w_out", (2, 1024, D))
    ident_d = din("ident", (128, 128))
    od_w_in = din("od_w_in", (2, D, 1712))
    od_peT = din("od_peT", (2, 128, 32))
    od_w1k = din("od_cmp_w1_k", (2, 2048, 128))
    od_w1v = din("od_cmp_w1_v", (2, 2048, 128))
    od_w2k = din("od_cmp_w2_k", (2, 128, 64))
    od_w2v = din("od_cmp_w2_v", (2, 128, 64))
    od_pool_w = din("od_pool_w", (2, 4, 64, 64))
    od_pool_scale = din("od_pool_scale", (2, 256))
    od_w_out = din("od_w_out", (2, 1280, D))
    rbx_d = din("rbx", (33, 16))
    ohp_d = din("ohp", (33, 384))
    shiftbig_d = din("shiftbig", (17, 248))
    mlt4_d = din("mlt4", (128, 512))
    bi_d = din("bi", (32, 2048))
    abig_d = din("abig", (128, 64))
    bbig_d = din("bbig", (128, 64))
    ov_d = din("ov", (128, 32))
    corrw_d = din("corrw", (128, 2, 16))
    tabM_t = nc.dram_tensor("tabM", [16, 128, 384], F32, kind="Internal")
    tabM = tabM_t.ap()
    out_d = nc.dram_tensor("out", [2, S, D], F32, kind="ExternalOutput").ap()
    modscr = nc.dram_tensor("modscr", [DEPTH, 2, 9 * D], F32, kind="Internal").ap()

    cx = Ctx(nc, es)

    uid = [0]

    def sb(name, shape, dt, stack=None):
        uid[0] += 1
        return (stack or es).enter_context(nc.sbuf_tensor("%s_%d" % (name, uid[0]), list(shape), dt))

    def ps(name, shape, dt, stack=None):
        return (stack or es).enter_context(nc.psum_tensor(name, list(shape), dt))

    x_sb = sb("x_sb", (128, NT, D), F32)
    bcA1 = sb("bcA1", (128, D), F32)
    bcA0 = sb("bcA0", (128, D), F32)
    bcGt = sb("bcGt", (128, D), F32)
    bcLg = sb("bcLg", (128, D), F32)
    bcLb = sb("bcLb", (128, D), F32)
    ident_bf = sb("ident_bf", (128, 128), BF16)
    ident_f = sb("ident_f", (128, 128), F32)
    onesm = sb("onesm", (128, 128), F32)
    ps_t = [ps("ps_t%d" % i, (128, 1024), BF16) for i in range(1)]
    ps_a = [ps("ps_a%d" % i, (128, 512), F32) for i in range(3)]
    ps_o = [ps("ps_o%d" % i, (128, 512), F32) for i in range(2)]
    po_i = [0]
    ps_y = [ps("ps_y%d" % i, (128, 1024), F32) for i in range(1)]
    pa_i = [0]

    def next_pa():
        i = pa_i[0]
        pa_i[0] = (i + 1) % 3
        return ps_a[i], ("ps_a", i)

    cx.dma("pool", ident_bf[:], ident_d[:, :], writes=["ident_bf"])
    cx.dma("sp", ident_f[:], ident_d[:, :], writes=["ident_f"])
    cx.op("dve", lambda e: e.memset(onesm[:], 1.0 / 512.0), writes=["onesm"])
    epst = sb("epst", (128, 1), F32)
    cx.op("dve", lambda e: e.memset(epst[:], EPS), writes=["epst"])

    with ExitStack() as st:
        cT_sb = sb("cT_sb", (128, 8, 2), F32, st)
        condT = sb("condT", (128, 8, 2), BF16, st)
        modrow = sb("modrow", (2, 9 * D), F32, st)
        adab = sb("adab", (2, 9 * D), F32, st)
        wa = [sb("wa%d" % i, (128, 8, 512), BF16, st) for i in range(3)]
        cx.dma("sp", cT_sb[:], cT_d.rearrange("(k p) b -> p k b", p=128), writes=["cT_sb"])
        cx.op("act", lambda e: e.activation(out=condT[:], in_=cT_sb[:], func=AF.Silu),
              reads=["cT_sb"], writes=["condT"])
        for l in range(DEPTH):
            cx.dma("sp", adab[:], ada_b[l:l + 1, :].partition_broadcast(2) if False else
                   ada_b[l:l + 1, :].to_broadcast([2, 9 * D]), writes=["adab"])
            for n in range(18):
                wi = (l * 18 + n) % 3
                cx.dma("pool", wa[wi][:], ada_w[l, :, n * 512:(n + 1) * 512].rearrange("(k p) n -> p k n", p=128),
                       writes=[("wa", wi)])
                pt, pk = next_pa()
                for k in range(8):
                    cx.op("pe", lambda e, k=k, pt=pt, wi=wi: e.matmul(pt[0:2, :], condT[:, k, :], wa[wi][:, k, :],
                                                                       start=(k == 0), stop=(k == 7)),
                          reads=["condT", ("wa", wi)], writes=[pk])
                cx.op("dve", lambda e, pt=pt, n=n: e.tensor_tensor(out=modrow[:, n * 512:(n + 1) * 512], in0=pt[0:2, :],
                                                                 in1=adab[:, n * 512:(n + 1) * 512], op=ALU.add),
                      reads=[pk, "adab"], writes=["modrow"])
            for sub in range(3):
                o = sub * 3072
                cx.op("dve", lambda e, o=o: e.tensor_scalar_add(out=modrow[:, o + 1024:o + 2048],
                                                               in0=modrow[:, o + 1024:o + 2048], scalar1=1.0),
                      reads=["modrow"], writes=["modrow"])
                cx.op("dve", lambda e, o=o, sub=sub: e.tensor_scalar(out=modrow[:, o + 2048:o + 3072],
                                                                   in0=modrow[:, o + 2048:o + 3072],
                                                                   scalar1=1.0, scalar2=RW[sub], op0=ALU.add, op1=ALU.mult),
                      reads=["modrow"], writes=["modrow"])
            cx.dma("sp", modscr[l], modrow[:], reads=["modrow"], writes=[("modscr", l)])
        cx.barrier()

    def load_bc(l, sub, b):
        o = sub * 3072
        cx.dma("sp", bcA0[:], modscr[l, b:b + 1, o:o + 1024].to_broadcast([128, D]), reads=[("modscr", l)], writes=["bcA0"])
        cx.dma("sp", bcA1[:], modscr[l, b:b + 1, o + 1024:o + 2048].to_broadcast([128, D]), reads=[("modscr", l)], writes=["bcA1"])
        cx.dma("sp", bcGt[:], modscr[l, b:b + 1, o + 2048:o + 3072].to_broadcast([128, D]), reads=[("modscr", l)], writes=["bcGt"])
        cx.dma("sp", bcLg[:], ln_g[l, sub:sub + 1, :].to_broadcast([128, D]), writes=["bcLg"])
        cx.dma("sp", bcLb[:], ln_b[l, sub:sub + 1, :].to_broadcast([128, D]), writes=["bcLb"])

    def adaln_T(ti, hT, hT_key, col0, tmpf, hb, par, aux="pool", tk="tmpf", ew_only=False):
        cx.op("dve", lambda e: e.tensor_tensor(out=tmpf[:], in0=x_sb[:, ti, :], in1=bcA1[:], op=ALU.mult),
              reads=[("x", ti), "bcA1"], writes=[tk])
        cx.op(aux, lambda e: e.tensor_tensor(out=hb[par][:], in0=tmpf[:], in1=bcA0[:], op=ALU.add),
              reads=[tk, "bcA0"], writes=[("hb", par)])
        if ew_only:
            return
        adaln_Tb(hT, hT_key, col0, hb, par)

    def adaln_Tb(hT, hT_key, col0, hb, par):
        for k in range(8):
            cx.op("pe", lambda e, k=k: e.transpose(ps_t[0][:, k * 128:(k + 1) * 128], hb[par][:, k * 128:(k + 1) * 128], ident_bf[:]),
                  reads=[("hb", par), "ident_bf"], writes=["ps_t"])
        cx.op("act", lambda e: e.copy(out=hT[:, :, col0:col0 + 128], in_=ps_t[0][:].rearrange("p (k n) -> p k n", k=8)),
              reads=["ps_t"], writes=[hT_key])

    def deepnorm_a(ti, zt, stats, mv, ysrc=None):
        if ysrc is None:
            cx.op("dve", lambda e: e.tensor_tensor(out=zt[:], in0=ps_y[0][:], in1=bcGt[:], op=ALU.mult),
                  reads=["ps_y", "bcGt"], writes=["zt"])
        else:
            for hh, (yap, yk) in enumerate(ysrc):
                cx.op("dve", lambda e, hh=hh, yap=yap: e.tensor_tensor(out=zt[:, hh * 512:(hh + 1) * 512], in0=yap, in1=bcGt[:, hh * 512:(hh + 1) * 512], op=ALU.mult),
                      reads=[yk, "bcGt", "zt"], writes=["zt"])
        cx.op("dve", lambda e: e.scalar_tensor_tensor(out=zt[:], in0=x_sb[:, ti, :], scalar=ALPHA, in1=zt[:],
                                                       op0=ALU.mult, op1=ALU.add),
              reads=[("x", ti), "zt"], writes=["zt"])
        for hh in range(2):
            cx.op("dve", lambda e, hh=hh: e.bn_stats(out=stats[:, hh, :], in_=zt[:, hh * 512:(hh + 1) * 512]),
                  reads=["zt"], writes=[("stats", hh)])
        cx.op("dve", lambda e: e.bn_aggr(out=mv[:], in_=stats[:]), reads=[("stats", 0), ("stats", 1)], writes=["mv"])

    def deepnorm_b(mv, rstd):
        cx.op("act", lambda e: e.activation(out=rstd[:], in_=mv[:, 1:2], func=AF.Ln, bias=epst[:, 0:1], scale=1.0),
              reads=["mv", "epst"], writes=["rstd"])
        cx.op("act", lambda e: e.activation(out=rstd[:], in_=rstd[:], func=AF.Exp, scale=-0.5), reads=["rstd"], writes=["rstd"])

    def deepnorm_c(ti, zt, mv, rstd, nmr, aux="pool", dve_norm=False):
        if dve_norm:
            cx.op("dve", lambda e: e.tensor_scalar(out=zt[:], in0=zt[:], scalar1=mv[:, 0:1], scalar2=rstd[:, 0:1], op0=ALU.subtract, op1=ALU.mult),
                  reads=["zt", "mv", "rstd"], writes=["zt"])
        else:
            cx.op("dve", lambda e: e.scalar_tensor_tensor(out=nmr[:], in0=mv[:, 0:1], scalar=-1.0, in1=rstd[:],
                                                          op0=ALU.mult, op1=ALU.mult),
                  reads=["mv", "rstd"], writes=["nmr"])
            cx.op("act", lambda e: e.activation(out=zt[:], in_=zt[:], func=AF.Identity, bias=nmr[:, 0:1], scale=rstd[:, 0:1]),
                  reads=["zt", "rstd", "nmr"], writes=["zt"])
        cx.op(aux, lambda e: e.tensor_tensor(out=zt[:], in0=zt[:], in1=bcLg[:], op=ALU.mult),
              reads=["zt", "bcLg"], writes=["zt"])
        cx.op("dve", lambda e: e.tensor_tensor(out=x_sb[:, ti, :], in0=zt[:], in1=bcLb[:], op=ALU.add),
              reads=["zt", "bcLb"], writes=[("x", ti)])

    def deepnorm(ti, zt, stats, mv, rstd, nmr, ysrc=None, aux="pool"):
        deepnorm_a(ti, zt, stats, mv, ysrc)
        deepnorm_b(mv, rstd)
        deepnorm_c(ti, zt, mv, rstd, nmr, aux)

    def small_tiles(st):
        return (sb("zt", (128, D), F32, st), sb("stats", (128, 2, 6), F32, st), sb("mv", (128, 2), F32, st),
                sb("rstd", (128, 1), F32, st), sb("nmr", (128, 1), F32, st))

    def ffn(l, f, sub, b):
        with ExitStack() as st:
            hT = [sb("hT%d" % i, (128, 8, 512), BF16, st) for i in range(2)]
            actT = sb("actT", (128, NJ, 512), BF16, st)
            wd = sb("wd", (128, NJ, D), BF16, st)
            NWB = 3
            wg = [sb("wg%d" % i, (128, 8, 256), BF16, st) for i in range(NWB)]
            wu = [sb("wu%d" % i, (128, 8, 256), BF16, st) for i in range(NWB)]
            tmpf = sb("tmpf", (128, D), F32, st)
            hb = [sb("hb%d" % i, (128, D), BF16, st) for i in range(2)]
            sg = [sb("sg%d" % i, (128, 512), F32, st) for i in range(2)]
            zt, stats, mv, rstd, nmr = small_tiles(st)
            load_bc(l, sub, b)
            blocks = [(g, jb) for g in range(4) for jb in range(11)]

            def load_w(bi):
                g, jb = blocks[bi]
                wi = bi % NWB
                cx.dma("pool", wg[wi][:], w_gate[l, f, :, jb * 256:(jb + 1) * 256].rearrange("(k p) n -> p k n", p=128),
                       writes=[("wg", wi)])
                cx.dma("pool", wu[wi][:], w_up[l, f, :, jb * 256:(jb + 1) * 256].rearrange("(k p) n -> p k n", p=128),
                       writes=[("wu", wi)])

            load_w(0)
            load_w(1)

            def load_wd(c0):
                c1 = min(NJ, c0 + 6)
                cx.dma("pool", wd[:, c0:c1, :], w_down[l, f, c0 * 128:c1 * 128, :].rearrange("(c p) d -> p c d", p=128),
                       writes=[("wd", c0)])
            wdk = [("wd", c0) for c0 in range(0, NJ, 6)]
            for t in range(4):
                adaln_T(t, hT[0], ("hT", 0), t * 128, tmpf, hb, t % 2, aux="dve")
            deferred = []
            for g in range(4):
                hc = hT[g % 2]
                hk = ("hT", g % 2)
                for jb in range(11):
                    bi = g * 11 + jb
                    if bi + 2 < len(blocks):
                        load_w(bi + 2)
                    if g == 0 and 1 <= jb <= 4:
                        load_wd((jb - 1) * 6)
                    if g + 1 < 4 and 3 <= jb <= 6:
                        t = jb - 3
                        adaln_Tb(hT[(g + 1) % 2], ("hT", (g + 1) % 2), t * 128, hb, t % 2)
                    if g + 1 < 4 and 1 <= jb <= 4:
                        t = jb - 1
                        adaln_T((g + 1) * 4 + t, hT[(g + 1) % 2], ("hT", (g + 1) % 2), t * 128, tmpf, hb, t % 2, aux="dve", ew_only=True)
                    wi = bi % NWB
                    for jj in range(2):
                        j = jb * 2 + jj
                        pg, pgk = next_pa()
                        pu, puk = next_pa()
                        for k in range(8):
                            cx.op("pe", lambda e, k=k, pg=pg, wi=wi, jj=jj: e.matmul(pg[:], wg[wi][:, k, jj * 128:(jj + 1) * 128], hc[:, k, :],
                                                                                    start=(k == 0), stop=(k == 7)),
                                  reads=[("wg", wi), hk], writes=[pgk])
                        for k in range(8):
                            cx.op("pe", lambda e, k=k, pu=pu, wi=wi, jj=jj: e.matmul(pu[:], wu[wi][:, k, jj * 128:(jj + 1) * 128], hc[:, k, :],
                                                                                    start=(k == 0), stop=(k == 7)),
                                  reads=[("wu", wi), hk], writes=[puk])
                        si = j % 2
                        cx.op("act", lambda e, pg=pg, si=si: e.activation(out=sg[si][:], in_=pg[:], func=AF.Silu),
                              reads=[pgk], writes=[("sg", si)])
                        cx.op("dve", lambda e, pu=pu, si=si, j=j: e.tensor_tensor(out=actT[:, j, :], in0=sg[si][:], in1=pu[:], op=ALU.mult),
                              reads=[puk, ("sg", si)], writes=[("actT", j)])
                    if deferred and deferred[0][0] <= jb:
                        deferred.pop(0)[1]()
                for t in range(4):
                    if t % 2 == 0:
                        ysrc = [(ps_y[0][:, 0:512], ("yh", 0)), (ps_y[0][:, 512:1024], ("yh", 1))]
                    else:
                        ysrc = [(ps_o[0][:], ("ps_o", 0)), (ps_o[1][:], ("ps_o", 1))]
                    for nh in range(2):
                        yap, yk = ysrc[nh]
                        for j in range(NJ):
                            cx.op("pe", lambda e, t=t, nh=nh, j=j, yap=yap: e.matmul(yap, actT[:, j, t * 128:(t + 1) * 128],
                                                                                   wd[:, j, nh * 512:(nh + 1) * 512], start=(j == 0), stop=(j == NJ - 1)),
                                  reads=[("actT", j)] + (wdk if j == 0 else []), writes=[yk])
                    if t == 3 and g < 3:
                        deferred.append((1, lambda g=g, t=t, ysrc=ysrc: deepnorm_a(g * 4 + t, zt, stats, mv, ysrc)))
                        deferred.append((2, lambda: deepnorm_b(mv, rstd)))
                        deferred.append((3, lambda g=g, t=t: deepnorm_c(g * 4 + t, zt, mv, rstd, nmr, aux="dve", dve_norm=True)))
                    else:
                        deepnorm(g * 4 + t, zt, stats, mv, rstd, nmr, ysrc=ysrc, aux="dve")
            cx.barrier()

    def even(l, b):
        jl = l // 2
        with ExitStack() as st:
            hT = [sb("hT0", (128, 8, 512), BF16, st)] * 2
            hb = [sb("hb0", (128, D), BF16, st)] * 2
            U = sb("U", (128, 4, 30 + 512), BF16, st)
            M = sb("M", (128, 2 + 512), F32, st)
            Mh = sb("Mh", (128, 4, 2), F32, st)
            GB = [sb("GB0", (128, 512), F32, st)] * 2
            CA = sb("CA", (128, 4, 512), F32, st)
            SQ = [sb("SQ0", (128, 512), F32, st)] * 2
            CB = sb("CB", (128, 512), F32, st)
            uzT = sb("uzT", (128, 8, 512), BF16, st)
            w5 = [sb("w5_%d" % i, (128, 8, 5, 128), BF16, st) for i in range(2)]
            wo = sb("wo", (128, 8, D), BF16, st)
            dg = sb("dg", (128, 4, 31, 128), BF16, st)
            caw = sb("caw", (128, 4, 31), F32, st)
            cab = sb("cab", (128, 4), F32, st)
            nag = sb("nag", (128, 4), F32, st)
            nab = sb("nab", (128, 4), F32, st)
            cbw = sb("cbw", (128, 4, 3), F32, st)
            sgm = sb("sgm", (128, 512), F32, st)
            tcp = sgm
            mean_sb = sb("mean_sb", (128, 512), F32, st)
            rs_sb = sb("rs_sb", (128, 512), F32, st)
            dtile = CB
            zt, stats, mv, rstd, nmr = small_tiles(st)
            tmpf = zt
            load_bc(l, 1, b)
            cx.dma("sp", caw[:], ev_caw[jl].rearrange("(c p) k -> p c k", p=128), writes=["caw"])
            cx.dma("sp", cbw[:], ev_cbw[jl].rearrange("(c p) k -> p c k", p=128), writes=["cbw"])
            for (tl, src, nm) in ((cab, ev_cab, "cab"), (nag, ev_nag, "nag"), (nab, ev_nab, "nab")):
                cx.dma("sp", tl[:], src[jl].rearrange("(c p) -> p c", p=128), writes=[nm], allow_slow_non_contiguous=True)
            w_in5 = ev_w_in[jl].rearrange("(k p) (g c n) -> p k g c n", p=128, g=5, c=4)
            chunks = [(g, c) for g in range(4) for c in range(4)]

            def load_w5(ci):
                g, c = chunks[ci]
                wi = ci % 2
                for gg in range(5):
                    cx.dma("pool", w5[wi][:, :, gg, :], w_in5[:, :, gg, c, :], writes=[("w5", wi, gg)])

            load_w5(0)
            cx.dma("pool", wo[:], ev_w_out[jl].rearrange("(c p) d -> p c d", p=128), writes=["wo"])
            cx.op("dve", lambda e: e.memset(U[:, :, 0:30], 0.0), writes=["Uh"])
            cx.op("dve", lambda e: e.memset(Mh[:], 0.0), writes=["Mh"])
            for t in range(4):
                adaln_T(t, hT[0], ("hT", 0), t * 128, tmpf, hb, 0, aux="dve", tk="zt")
            deferred = []
            for g in range(4):
                hc, hk = hT[0], ("hT", 0)
                if deferred and deferred[0][0] < 0:
                    deferred.pop(0)[1]()
                for c in range(4):
                    ci = g * 4 + c
                    wi = ci % 2
                    if ci + 1 < len(chunks):
                        load_w5(ci + 1)
                    gbi = 0
                    if g == 0:
                        for k in range(31):
                            cx.op("dve", lambda e, c=c, k=k: e.tensor_scalar(out=dg[:, c, k, :], in0=ident_f[:], scalar1=caw[:, c, k:k + 1], scalar2=None, op0=ALU.mult),
                                  reads=["ident_f", "caw"], writes=[("dg", c)])
                    for gg in (1, 0, 3, 4, 2):
                        pt, pk = next_pa()
                        for k in range(8):
                            cx.op("pe", lambda e, k=k, pt=pt, wi=wi, gg=gg: e.matmul(pt[:], w5[wi][:, k, gg, :], hc[:, k, :],
                                                                                    start=(k == 0), stop=(k == 7)),
                                  reads=[("w5", wi, gg), hk], writes=[pk])
                        if gg == 1:
                            cx.op("act", lambda e, pt=pt: e.activation(out=sgm[:], in_=pt[:], func=AF.Sigmoid),
                                  reads=[pk], writes=["sgm"])
                        elif gg == 0:
                            cx.op("dve", lambda e, pt=pt, c=c: e.tensor_tensor(out=U[:, c, 30:542], in0=pt[:], in1=sgm[:], op=ALU.mult),
                                  reads=[pk, "sgm"], writes=[("U", c)])
                        elif gg == 3:
                            cx.op("act", lambda e, pt=pt: e.copy(out=tcp[:], in_=pt[:]), reads=[pk], writes=["sgm"])
                        elif gg == 4:
                            cx.op("dve", lambda e, pt=pt, c=c: e.tensor_tensor(out=M[:, 2:514], in0=pt[:], in1=tcp[:], op=ALU.mult),
                                  reads=[pk, "sgm"], writes=["M"])
                            cx.op("dve", lambda e, c=c: e.tensor_copy(out=M[:, 0:2], in_=Mh[:, c, :]), reads=["Mh"], writes=["M0"])
                        else:
                            cx.op("act", lambda e, pt=pt, gbi=gbi: e.copy(out=GB[gbi][:], in_=pt[:]), reads=[pk], writes=[("GB", gbi)])
                    pc, pck = ps_o[ci % 2], ("ps_o", ci % 2)
                    for k in range(31):
                        cx.op("pe", lambda e, c=c, k=k, pc=pc: e.matmul(pc[:], dg[:, c, k, :], U[:, c, k:k + 512], start=(k == 0), stop=(k == 30)),
                              reads=[("U", c), "Uh", ("dg", c)], writes=[pck])
                    cx.op("act", lambda e, c=c, pc=pc: e.activation(out=CA[:, c, :], in_=pc[:], func=AF.Identity, bias=cab[:, c:c + 1], scale=1.0),
                          reads=[pck, "cab"], writes=[("CA", c)])
                    cx.op("dve", lambda e, c=c: e.tensor_scalar(out=CB[:], in0=M[:, 0:512], scalar1=cbw[:, c, 0:1], scalar2=None, op0=ALU.mult),
                          reads=["M", "M0", "cbw"], writes=["CB"])
                    for k in range(1, 3):
                        cx.op("dve", lambda e, c=c, k=k: e.scalar_tensor_tensor(out=CB[:], in0=M[:, k:k + 512], scalar=cbw[:, c, k:k + 1],
                                                                               in1=CB[:], op0=ALU.mult, op1=ALU.add),
                              reads=["M", "M0", "CB"], writes=["CB"])
                    cx.op("dve", lambda e, c=c: e.tensor_copy(out=Mh[:, c, :], in_=M[:, 512:514]), reads=["M"], writes=["Mh"])
                    cx.op("dve", lambda e, c=c, gbi=gbi: e.tensor_tensor(out=uzT[:, 4 + c, :], in0=CB[:], in1=GB[gbi][:], op=ALU.mult),
                          reads=["CB", ("GB", gbi)], writes=[("uzT", 4 + c)])
                    if deferred and deferred[0][0] <= c:
                        deferred.pop(0)[1]()
                if g + 1 < 4:
                    for t in range(4):
                        adaln_T((g + 1) * 4 + t, hT[0], ("hT", 0), t * 128, tmpf, hb, 0, aux="dve", tk="zt")
                cak = [("CA", c) for c in range(4)]
                pm, pmk = next_pa()
                pq, pqk = next_pa()
                for c in range(4):
                    cx.op("pe", lambda e, c=c: e.matmul(pm[:], onesm[:], CA[:, c, :], start=(c == 0), stop=(c == 3)),
                          reads=cak + ["onesm"], writes=[pmk])
                for c in range(4):
                    cx.op("act", lambda e, c=c: e.activation(out=SQ[c % 2][:], in_=CA[:, c, :], func=AF.Square), reads=[("CA", c)], writes=[("SQ", 0)])
                    cx.op("pe", lambda e, c=c: e.matmul(pq[:], onesm[:], SQ[c % 2][:], start=(c == 0), stop=(c == 3)),
                          reads=[("SQ", 0), "onesm"], writes=[pqk])
                cx.op("act", lambda e: e.copy(out=mean_sb[:], in_=pm[:]), reads=[pmk], writes=["mean_sb"])
                cx.op("dve", lambda e: e.tensor_tensor(out=rs_sb[:], in0=mean_sb[:], in1=mean_sb[:], op=ALU.mult),
                      reads=["mean_sb"], writes=["rs_sb"])
                cx.op("dve", lambda e: e.tensor_tensor(out=rs_sb[:], in0=pq[:], in1=rs_sb[:], op=ALU.subtract),
                      reads=[pqk, "rs_sb"], writes=["rs_sb"])
                cx.op("act", lambda e: e.activation(out=rs_sb[:], in_=rs_sb[:], func=AF.Ln, bias=epst[:, 0:1], scale=1.0),
                      reads=["rs_sb", "epst"], writes=["rs_sb"])
                cx.op("act", lambda e: e.activation(out=rs_sb[:], in_=rs_sb[:], func=AF.Exp, scale=-0.5), reads=["rs_sb"], writes=["rs_sb"])
                for c in range(4):
                    cx.op("dve", lambda e, c=c: e.tensor_tensor(out=dtile[:], in0=CA[:, c, :], in1=mean_sb[:], op=ALU.subtract),
                          reads=[("CA", c), "mean_sb"], writes=["CB"])
                    cx.op("dve", lambda e: e.tensor_tensor(out=dtile[:], in0=dtile[:], in1=rs_sb[:], op=ALU.mult),
                          reads=["CB", "rs_sb"], writes=["CB"])
                    cx.op("act", lambda e, c=c: e.activation(out=uzT[:, c, :], in_=dtile[:], func=AF.Silu, bias=nab[:, c:c + 1], scale=nag[:, c:c + 1]),
                          reads=["CB", "nab", "nag"], writes=[("uzT", c)])
                cx.op("dve", lambda e: e.tensor_copy(out=U[:, :, 0:30], in_=U[:, :, 512:542]),
                      reads=[("U", c) for c in range(4)], writes=["Uh"])
                uzk = [("uzT", c) for c in range(8)]
                for t in range(4):
                    if t % 2 == 0:
                        ysrc = [(ps_y[0][:, 0:512], ("yh", 0)), (ps_y[0][:, 512:1024], ("yh", 1))]
                    else:
                        ysrc = [(ps_o[0][:], ("ps_o", 0)), (ps_o[1][:], ("ps_o", 1))]
                    for nh in range(2):
                        yap, yk = ysrc[nh]
                        for ci_, c in enumerate((4, 5, 6, 7, 0, 1, 2, 3)):
                            cx.op("pe", lambda e, t=t, nh=nh, c=c, ci_=ci_, yap=yap: e.matmul(yap, uzT[:, c, t * 128:(t + 1) * 128],
                                                                                            wo[:, c, nh * 512:(nh + 1) * 512], start=(ci_ == 0), stop=(ci_ == 7)),
                                  reads=[("uzT", c), "wo"], writes=[yk])
                    if t == 3 and g < 3:
                        deferred.append((-1, lambda g=g, t=t, ysrc=ysrc: deepnorm_a(g * 4 + t, zt, stats, mv, ysrc)))
                        deferred.append((0, lambda: deepnorm_b(mv, rstd)))
                        deferred.append((1, lambda g=g, t=t: deepnorm_c(g * 4 + t, zt, mv, rstd, nmr, aux="dve", dve_norm=True)))
                    else:
                        deepnorm(g * 4 + t, zt, stats, mv, rstd, nmr, ysrc=ysrc, aux="dve")
            cx.barrier()

    def build_tables():
        with ExitStack() as st:
            rbx_sb = sb("rbx_sb", (33, 16), F32, st)
            ohp_sb = sb("ohp_sb", (33, 384), F32, st)
            rbb = sb("rbb", (33, 16, 128), F32, st)
            tabS = [sb("tabS%d" % i, (128, 384), F32, st) for i in range(2)]
            cx.dma("sp", rbx_sb[:], rbx_d[:, :], writes=["rbx_sb"])
            cx.dma("sp", ohp_sb[:], ohp_d[:, :], writes=["ohp_sb"])
            cx.op("dve", lambda e: e.tensor_copy(out=rbb[:], in_=rbx_sb[:].unsqueeze(2).to_broadcast([33, 16, 128])),
                  reads=["rbx_sb"], writes=["rbb"])
            for h in range(16):
                pt, pk = next_pa()
                cx.op("pe", lambda e, h=h, pt=pt: e.matmul(pt[:, 0:384], rbb[:, h, :], ohp_sb[:], start=True, stop=True),
                      reads=["rbb", "ohp_sb"], writes=[pk])
                cx.op("act", lambda e, h=h, pt=pt: e.copy(out=tabS[h % 2][:], in_=pt[:, 0:384]), reads=[pk], writes=[("tabS", h % 2)])
                cx.dma("sp", tabM[h], tabS[h % 2][:], reads=[("tabS", h % 2)], writes=["tabM"])
            cx.barrier()

    def skew(off, pstride, nparts):
        return bass.AP(tabM_t, off, [[pstride, nparts], [2 * 128 * 384, 4], [1, 128]])

    def odd(l, b):
        jl = l // 2
        with ExitStack() as so:
            kvcT = sb("kvcT", (128, S), BF16, so)
            ksA = sb("ksA", (128, S), BF16, so)
            ksB = sb("ksB", (128, S), BF16, so)
            kwA = sb("kwA", (128, S), BF16, so)
            kwB = sb("kwB", (128, S), BF16, so)
            vsw = sb("vsw", (128, NT, 2, 65), BF16, so)
            ypT = sb("ypT", (128, 2, S), BF16, so)
            biasP = sb("biasP", (128, 4, 4, 128), BF16, so)
            biasD = sb("biasD", (128, 4, 4, 128), BF16, so)
            Bc = sb("Bc", (17, 4, 4, 128), BF16, so)
            shiftb = sb("shiftb", (17, 248), BF16, so)
            mlt4 = sb("mlt4", (128, 512), BF16, so)
            bi_sb = sb("bi_sb", (128, S), BF16, so)
            abig = sb("abig", (128, 64), F32, so)
            bbig = sb("bbig", (128, 64), F32, so)
            kcA = sb("kcA", (128, 128), BF16, so)
            kcB = sb("kcB", (128, 128), BF16, so)
            vcx = sb("vcx", (128, 97), BF16, so)
            zt, stats, mv, rstd, nmr = small_tiles(so)
            tmpf = zt
            hb = [sb("hb0", (128, D), BF16, so)] * 2
            load_bc(l, 1, b)
            win = od_w_in[jl]
            with ExitStack() as st:
                hTA = [sb("hTA%d" % i, (128, 8, 512), BF16, st) for i in range(2)]
                wA = sb("wA", (128, 8, 768), BF16, st)
                uP = sb("uP", (128, 2, 527), F32, st)
                s2 = sb("s2", (128, 2, 527), F32, st)
                s4 = sb("s4", (128, 2, 527), F32, st)
                ssel = sb("ssel", (128, 2, 512), F32, st)
                dP = sb("dP", (128, 2, 512), BF16, st)
                wbd = sb("wbd", (128, 2, 128), BF16, st)
                pscale = sb("pscale", (128, 2), F32, st)
                corrw = sb("corrw", (128, 2, 16), F32, st)
                w1 = sb("w1", (128, 32, 128), BF16, st)
                peT = sb("peT", (128, 32), BF16, st)
                w2k = sb("w2k", (128, 128), BF16, st)
                w2v = sb("w2v", (128, 64), BF16, st)
                c1 = sb("c1", (128, 2), F32, st)
                hid = sb("hid", (128, 2, 128), BF16, st)

                def wcols(dst0, src0, n):
                    cx.dma("pool", wA[:, :, dst0:dst0 + n], win[:, src0:src0 + n].rearrange("(k p) n -> p k n", p=128),
                           writes=[("wA", dst0)])
                wcols(0, 1024, 128)
                wcols(128, 1152, 64); wcols(192, 1152, 64)
                wcols(256, 1280, 64); wcols(320, 1280, 64)
                wcols(384, 1456, 256)
                wcols(640, 1216, 64); wcols(704, 1344, 64)
                wAk = [("wA", d0) for d0 in (0, 128, 192, 256, 320, 384, 640, 704)]
                for t in range(4):
                    adaln_T(t, hTA[0], ("hT", 0), t * 128, tmpf, hb, 0, tk="zt")
                cx.op("dve", lambda e: e.memset(Bc[:], NEGB), writes=["Bc"])
                for hg in range(4):
                    h0 = 8 * (hg // 2) + (hg % 2)
                    cx.dma("pool", biasP[:, hg, :, :], skew(255 + h0 * 49152, 383, 128), reads=["tabM"], writes=["biasP"])
                    cx.dma("pool", biasD[:, hg, :, :], skew(127 + h0 * 49152, 383, 128), reads=["tabM"], writes=["biasD"])
                    cx.dma("pool", Bc[0:16, hg, :, :], skew(240 + h0 * 49152, 368, 16), reads=["tabM"], writes=["Bc"])
                for nm, tl in (("ksA", ksA), ("ksB", ksB), ("kwA", kwA), ("kwB", kwB), ("bi_sb", bi_sb), ("kcA", kcA), ("kcB", kcB)):
                    cx.op("dve", lambda e, tl=tl: e.memset(tl[:], 0.0), writes=[nm])
                cx.dma("pool", shiftb[:], shiftbig_d[:, :], writes=["shiftb"])
                cx.dma("pool", mlt4[:], mlt4_d[:, :], writes=["mlt4"])
                cx.dma("pool", bi_sb[0:32, :], bi_d[:, :], writes=["bi_sb"])
                cx.dma("sp", abig[:], abig_d[:, :], writes=["abig"])
                cx.dma("sp", bbig[:], bbig_d[:, :], writes=["bbig"])
                cx.op("dve", lambda e: e.memset(vsw[:, :, :, 64:65], 1.0), writes=["vsw1"])
                cx.op("dve", lambda e: e.memset(vcx[:], 0.0), writes=["vcx"])
                cx.op("dve", lambda e: e.memset(vcx[:, 64:65], 1.0), reads=[], writes=["vcx"])
                cx.dma("pool", vcx[:, 65:97], ov_d[:, :], writes=["vcx"])

                cx.op("dve", lambda e: e.memset(wbd[:], 0.0), writes=["wbd"])
                for gi in range(4):
                    hh, cc = gi % 2, gi // 2
                    cx.dma("pool", wbd[hh * 64:(hh + 1) * 64, cc, hh * 64:(hh + 1) * 64], od_pool_w[jl, gi], reads=[], writes=["wbd"])
                cx.dma("sp", pscale[:], od_pool_scale[jl].rearrange("(c p) -> p c", p=128), writes=["pscale"], allow_slow_non_contiguous=True)
                cx.dma("sp", corrw[:], corrw_d[:, :, :], writes=["corrw"])
                cx.dma("pool", w1[0:64, :, :], od_w1k[jl].rearrange("(l d) m -> d l m", d=64), writes=["w1k"])
                cx.dma("pool", w1[64:128, :, :], od_w1v[jl].rearrange("(l d) m -> d l m", d=64), writes=["w1v"])
                cx.dma("pool", peT[:], od_peT[jl], writes=["peT"])
                cx.dma("pool", w2k[:, 0:64], od_w2k[jl], writes=["w2k"])
                cx.dma("pool", w2k[:, 64:128], od_w2k[jl], writes=["w2k"])
                cx.dma("pool", w2v[:], od_w2v[jl], writes=["w2v"])
                cx.op("dve", lambda e: e.memset(uP[:, :, 0:15], 0.0), writes=["uPh"])
                cx.op("dve", lambda e: e.memset(hid[:], 0.0), writes=["hid"])
                for g in range(4):
                    hT, hTk = hTA[g % 2], ("hT", g % 2)
                    gs = slice(g * 512, (g + 1) * 512)
                    for ch in range(5):
                        pt, pk = next_pa()
                        for k in range(8):
                            cx.op("pe", lambda e, k=k, pt=pt, ch=ch: e.matmul(pt[:], wA[:, k, ch * 128:(ch + 1) * 128], hT[:, k, :],
                                                                            start=(k == 0), stop=(k == 7)),
                                  reads=wAk + [hTk], writes=[pk])
                        if ch == 0:
                            cx.op("act", lambda e, pt=pt: e.copy(out=kvcT[:, gs], in_=pt[:]), reads=[pk], writes=[("kvcT", g)])
                        elif ch in (1, 2):
                            tA, tB, nm = (ksA, ksB, "ks") if ch == 1 else (kwA, kwB, "kw")
                            cx.op("act", lambda e, pt=pt, tA=tA: e.copy(out=tA[0:64, gs], in_=pt[0:64, :]), reads=[pk, nm + "A"], writes=[nm + "A"])
                            cx.op("dve", lambda e, pt=pt, tB=tB: e.tensor_copy(out=tB[64:128, gs], in_=pt[64:128, :]), reads=[pk, nm + "B"], writes=[nm + "B"])
                        else:
                            cx.op("act", lambda e, pt=pt, ch=ch: e.copy(out=uP[:, ch - 3, 15:527], in_=pt[:]), reads=[pk], writes=[("uP", ch - 3)])
                    for t in range(4):
                        ti = g * 4 + t
                        pt, pk = next_pa()
                        for k in range(8):
                            cx.op("pe", lambda e, k=k, pt=pt, t=t: e.matmul(pt[:, 0:128], hT[:, k, t * 128:(t + 1) * 128], wA[:, k, 640:768],
                                                                          start=(k == 0), stop=(k == 7)),
                                  reads=wAk + [hTk], writes=[pk])
                        cx.op("act", lambda e, pt=pt, ti=ti: e.copy(out=vsw[:, ti, :, 0:64], in_=pt[:, 0:128].rearrange("p (a d) -> p a d", a=2)),
                              reads=[pk], writes=[("vsw", ti)])
                    if g + 1 < 4:
                        for t in range(4):
                            adaln_T((g + 1) * 4 + t, hTA[(g + 1) % 2], ("hT", (g + 1) % 2), t * 128, tmpf, hb, 0, tk="zt")
                    uk = [("uP", 0), ("uP", 1), "uPh"]
                    cx.op("dve", lambda e: e.tensor_tensor(out=s2[:, :, 1:527], in0=uP[:, :, 1:527], in1=uP[:, :, 0:526], op=ALU.add),
                          reads=uk, writes=["s2"])
                    cx.op("dve", lambda e: e.tensor_scalar(out=ssel[0:64, 0, :], in0=s2[0:64, 0, 15:527], scalar1=0.5, scalar2=None, op0=ALU.mult),
                          reads=["s2"], writes=["ssel"])
                    cx.op("dve", lambda e: e.tensor_tensor(out=s4[:, :, 3:527], in0=s2[:, :, 3:527], in1=s2[:, :, 1:525], op=ALU.add),
                          reads=["s2"], writes=["s4"])
                    cx.op("dve", lambda e: e.tensor_scalar(out=ssel[64:128, 0, :], in0=s4[64:128, 0, 15:527], scalar1=0.25, scalar2=None, op0=ALU.mult),
                          reads=["s4"], writes=["ssel"])
                    cx.op("dve", lambda e: e.tensor_tensor(out=s2[:, :, 7:527], in0=s4[:, :, 7:527], in1=s4[:, :, 3:523], op=ALU.add),
                          reads=["s4", "ssel"], writes=["s2"])
                    cx.op("dve", lambda e: e.tensor_scalar(out=ssel[0:64, 1, :], in0=s2[0:64, 1, 15:527], scalar1=0.125, scalar2=None, op0=ALU.mult),
                          reads=["s2"], writes=["ssel"])
                    cx.op("dve", lambda e: e.tensor_tensor(out=s4[:, :, 15:527], in0=s2[:, :, 15:527], in1=s2[:, :, 7:519], op=ALU.add),
                          reads=["s2", "ssel"], writes=["s4"])
                    cx.op("dve", lambda e: e.tensor_scalar(out=ssel[64:128, 1, :], in0=s4[64:128, 1, 15:527], scalar1=0.0625, scalar2=None, op0=ALU.mult),
                          reads=["s4"], writes=["ssel"])
                    if g == 0:
                        cx.op("dve", lambda e: e.tensor_tensor(out=ssel[:, :, 0:16], in0=ssel[:, :, 0:16], in1=corrw[:], op=ALU.mult),
                              reads=["ssel", "corrw"], writes=["ssel"])
                    cx.op("dve", lambda e: e.tensor_tensor(out=dP[:], in0=ssel[:], in1=uP[:, :, 15:527], op=ALU.subtract),
                          reads=["ssel"] + uk, writes=["dP"])
                    for c in range(2):
                        pt, pk = next_pa()
                        cx.op("pe", lambda e, c=c, pt=pt: e.matmul(pt[:], wbd[:, c, :], dP[:, c, :], start=True, stop=True),
                              reads=["wbd", "dP"], writes=[pk])
                        cx.op("act", lambda e, c=c, pt=pt: e.activation(out=ypT[:, c, gs], in_=pt[:], func=AF.Copy, scale=pscale[:, c:c + 1]),
                              reads=[pk, "pscale"], writes=[("ypT", g)])
                    cx.op("pool", lambda e: e.tensor_copy(out=uP[:, :, 0:15], in_=uP[:, :, 512:527]), reads=uk + ["dP", "s2"], writes=["uPh"])
                kvk = [("kvcT", g) for g in range(4)]
                for br in range(2):
                    P0 = slice(br * 64, (br + 1) * 64)
                    ph, phk = next_pa()
                    pc, pck = next_pa()
                    for li in range(32):
                        cx.op("pe", lambda e, li=li: e.matmul(ph[:, 0:127], w1[P0, li, :], kvcT[P0, li:li + 2017:16], start=(li == 0), stop=(li == 31)),
                              reads=kvk + ["w1k", "w1v"], writes=[phk])
                    for li in range(32):
                        cx.op("pe", lambda e, li=li: e.matmul(pc[:, 0:1], w1[P0, li, :], peT[P0, li:li + 1], start=(li == 0), stop=(li == 31)),
                              reads=["peT", "w1k", "w1v"], writes=[pck])
                    cx.op("act", lambda e: e.copy(out=c1[:, br:br + 1], in_=pc[:, 0:1]), reads=[pck], writes=[("c1", br)])
                    cx.op("act", lambda e: e.activation(out=hid[:, br, 0:127], in_=ph[:, 0:127], func=AF.Silu, bias=c1[:, br:br + 1], scale=1.0),
                          reads=[phk, ("c1", br), "hid"], writes=[("hid", br)])
                    po, pok = next_pa()
                    if br == 0:
                        cx.op("pe", lambda e: e.matmul(po[:, 0:127], w2k[:], hid[:, 0, 0:127], start=True, stop=True),
                              reads=["w2k", ("hid", 0)], writes=[pok])
                        cx.op("act", lambda e: e.copy(out=kcA[0:64, 0:127], in_=po[0:64, 0:127]), reads=[pok, "kcA"], writes=["kcA"])
                        cx.op("act", lambda e: e.copy(out=kcB[64:128, 0:127], in_=po[64:128, 0:127]), reads=[pok, "kcB"], writes=["kcB"])
                    else:
                        cx.op("pe", lambda e: e.matmul(po[0:127, 0:64], hid[:, 1, 0:127], w2v[:], start=True, stop=True),
                              reads=["w2v", ("hid", 1)], writes=[pok])
                        cx.op("act", lambda e: e.copy(out=vcx[0:127, 0:64], in_=po[0:127, 0:64]), reads=[pok, "vcx"], writes=["vcx"])
                cx.barrier()
            with ExitStack() as st:
                wq = sb("wq", (128, 8, 1024), BF16, st)
                wgt = sb("wgt", (128, 8, 48), BF16, st)
                wo = sb("wo_o", (128, 10, D), BF16, st)
                hTq = [sb("hTq%d" % i, (128, 8, 128), BF16, st) for i in range(2)]
                qT = [sb("qT%d" % i, (128, 8, 128), BF16, st) for i in range(2)]
                gsig = [sb("gsig%d" % i, (128, 48), F32, st) for i in range(2)]
                PT = [sb("PT%d" % i, (128, 512), BF16, st) for i in range(3)]
                acc_o = sb("acc_o", (128, D), F32, st)
                tmpo = [sb("tmpo0", (128, 4, 64), F32, st)] * 2
                tmpi = sb("tmpi", (128, 4, 32), F32, st)
                rden = [sb("rden%d" % i, (128, 4), F32, st) for i in range(2)]
                fac = [sb("fac%d" % i, (128, 4), F32, st) for i in range(2)]
                imph = sb("imph", (128, 4, 32), F32, st)
                impp = sb("impp", (128, 32), F32, st)
                cmp3 = sb("cmp3", (128, 32, 32), BF16, st)
                rank = sb("rank", (128, 32), F32, st)
                selT = sb("selT", (128, 4, 128), BF16, st)
                ob = sb("ob", (128, D), BF16, st)
                ocT = sb("ocT", (128, 8, 128), BF16, st)
                tmpfC = sb("tmpfC", (128, D), F32, st)
                cx.dma("pool", wq[:], win[:, 0:1024].rearrange("(k p) n -> p k n", p=128), writes=["wq"])
                cx.dma("pool", wgt[:], win[:, 1408:1456].rearrange("(k p) n -> p k n", p=128), writes=["wgt"])
                cx.dma("pool", wo[:], od_w_out[jl].rearrange("(c p) d -> p c d", p=128), writes=["wo"])
                acc_v = acc_o[:].rearrange("p (c r d) -> p c r d", r=2, d=64)
                cx.op("dve", lambda e: e.memset(selT[:], 0.0), writes=["selT"])
                ptc = [0]
                uc = [0]

                def prep_a(qb):
                    cur = qb % 2
                    adaln_T(qb, hTq[cur], ("hTq", cur), 0, tmpfC, hb, 0, tk="tmpfC", ew_only=True)

                def prep(qb):
                    cur = qb % 2
                    adaln_Tb(hTq[cur], ("hTq", cur), 0, hb, 0)
                    for half in range(2):
                        pt, pk = ps_y[0][:, half * 512:(half + 1) * 512], ("yh", half)
                        for c4 in range(4):
                            c = half * 4 + c4
                            for k in range(8):
                                cx.op("pe", lambda e, k=k, c=c, c4=c4, pt=pt: e.matmul(pt[:, c4 * 128:(c4 + 1) * 128], wq[:, k, c * 128:(c + 1) * 128], hTq[cur][:, k, :],
                                                                                     start=(k == 0), stop=(k == 7)),
                                      reads=["wq", ("hTq", cur)], writes=[pk])
                        cx.op("dve", lambda e, pt=pt, half=half: e.tensor_scalar(out=qT[cur][:, half * 4:half * 4 + 4, :], in0=pt.rearrange("p (c n) -> p c n", c=4),
                                                                               scalar1=0.125, scalar2=None, op0=ALU.mult),
                              reads=[pk], writes=[("qT", cur)])
                    pt, pk = ps_y[0][:, 0:512], ("yh", 0)
                    for k in range(8):
                        cx.op("pe", lambda e, k=k, pt=pt: e.matmul(pt[:, 0:48], hTq[cur][:, k, :], wgt[:, k, :], start=(k == 0), stop=(k == 7)),
                              reads=["wgt", ("hTq", cur)], writes=[pk])
                    cx.op("act", lambda e, pt=pt: e.activation(out=gsig[cur][:], in_=pt[:, 0:48], func=AF.Exp, scale=-1.0), reads=[pk], writes=[("gsig", cur)])
                    cx.op("dve", lambda e: e.tensor_scalar_add(out=gsig[cur][:], in0=gsig[cur][:], scalar1=1.0), reads=[("gsig", cur)], writes=[("gsig", cur)])
                    cx.op("dve", lambda e: e.reciprocal(out=gsig[cur][:], in_=gsig[cur][:]), reads=[("gsig", cur)], writes=[("gsig", cur)])

                def topk_dve(qb):
                    cx.op("dve", lambda e: e.tensor_reduce(out=impp[:], in_=imph[:].rearrange("p g s -> p s g"), axis=AX.X, op=ALU.add),
                          reads=[("imph", g) for g in range(4)], writes=["impp"])
                    cx.op("dve", lambda e: e.tensor_tensor(out=impp[:], in0=impp[:], in1=abig[:, 32 - 2 * qb:64 - 2 * qb], op=ALU.mult),
                          reads=["impp", "abig"], writes=["impp"])
                    cx.op("dve", lambda e: e.tensor_tensor(out=impp[:], in0=impp[:], in1=bbig[:, 32 - 2 * qb:64 - 2 * qb], op=ALU.add),
                          reads=["impp", "bbig"], writes=["impp"])
                    cx.op("dve", lambda e: e.memset(impp[:, 0:1], 3e30), reads=["impp"], writes=["impp"])
                    cx.op("dve", lambda e: e.tensor_tensor(out=cmp3[:], in0=impp[:].unsqueeze(1).to_broadcast([128, 32, 32]),
                                                           in1=impp[:].unsqueeze(2).to_broadcast([128, 32, 32]), op=ALU.is_gt),
                          reads=["impp"], writes=["cmp3"])
                    cx.op("dve", lambda e: e.tensor_reduce(out=rank[:], in_=cmp3[:], axis=AX.X, op=ALU.add), reads=["cmp3"], writes=["rank"])
                    cx.op("dve", lambda e: e.tensor_scalar(out=rank[:], in0=rank[:], scalar1=15.5, scalar2=-1.0, op0=ALU.is_gt, op1=ALU.mult),
                          reads=["rank"], writes=["rank"])

                def topk_pe():
                    pt, pk = ps_y[0][:, 512:1024], ("yh", 1)
                    cx.op("pe", lambda e, pt=pt: e.transpose(pt[0:32, 0:128], rank[:], ident_f[:]), reads=["rank", "ident_f"], writes=[pk])
                    cx.op("act", lambda e, pt=pt: e.copy(out=selT[0:32, :, :], in_=pt[0:32, 0:128].unsqueeze(1).to_broadcast([32, 4, 128])),
                          reads=[pk, "selT"], writes=["selT"])

                def attention(qb):
                    cur = qb % 2
                    gs_v = gsig[cur][:].rearrange("p (c r b) -> p c r b", r=2, b=3)
                    units = []
                    for br in (0, 2, 1):
                        for hg in range(4):
                            if br == 0:
                                kbs = [0]
                            elif br == 1:
                                kbs = list(range(0, qb + 1))
                            else:
                                kbs = list(range(max(0, qb - 4), qb + 1))
                            units.append((br, hg, kbs))
                    steps = [(u, ii) for u, (br, hg, kbs) in enumerate(units) for ii in range(len(kbs))]
                    pts = {}
                    pos = {}

                    def emit_S(si):
                        u, ii = steps[si]
                        br, hg, kbs = units[u]
                        kb = kbs[ii]
                        if br == 1 and hg == 0 and ii == 0 and qb >= 8:
                            topk_pe()
                        par, half = hg % 2, hg // 2
                        P0 = slice(par * 64, (par + 1) * 64)
                        hs = slice(4 * half, 4 * half + 4)
                        rhs_q = qT[cur][:, hs, :]
                        ex = []
                        if br == 0:
                            lhs = (kcA if par == 0 else kcB)[:, :]
                            ex.append((shiftb[:, 120 - 8 * qb:248 - 8 * qb], Bc[:, hg, :, :], ["shiftb", "Bc"]))
                        else:
                            kT_t = ((ksA, ksB) if br == 1 else (kwA, kwB))[par]
                            lhs = kT_t[:, kb * 128:(kb + 1) * 128]
                            if kb == qb:
                                ex.append((ident_bf[:], biasD[:, hg, :, :], ["ident_bf", "biasD"]))
                            if kb == qb - 1:
                                ex.append((ident_bf[:], biasP[:, hg, :, :], ["ident_bf", "biasP"]))
                            if br == 2 and kb == qb - 4:
                                ex.append((ident_bf[:], mlt4[:], ["ident_bf", "mlt4"]))
                            if br == 1 and qb >= 8:
                                ex.append((bi_sb[:, kb * 128:(kb + 1) * 128], selT[:].rearrange("p a t -> p (a t)"), ["bi_sb", "selT"]))
                        pt, pk = next_pa()
                        pts[si] = (pt, pk)
                        cx.op("pe", lambda e: e.matmul(pt[:], lhs, rhs_q, start=True, stop=(len(ex) == 0)),
                              reads=[("qT", cur), "kcA", "kcB"], writes=[pk])
                        for xi, (lh, rh, rk) in enumerate(ex):
                            cx.op("pe", lambda e, lh=lh, rh=rh, xi=xi: e.matmul(pt[:], lh, rh, start=False, stop=(xi == len(ex) - 1)),
                                  reads=rk, writes=[pk])

                    LOOK = 2
                    for si in range(min(LOOK, len(steps))):
                        emit_S(si)
                    for si, (u, ii) in enumerate(steps):
                        br, hg, kbs = units[u]
                        kb = kbs[ii]
                        par, half = hg % 2, hg // 2
                        hs = slice(4 * half, 4 * half + 4)
                        if si + LOOK < len(steps):
                            emit_S(si + LOOK)
                        if ii == 0:
                            oi = uc[0] % 2
                            uc[0] += 1
                            pos[u] = oi
                        oi = pos[u]
                        po, pok = ps_o[oi], ("ps_o", oi)
                        pov = po[:].rearrange("p (i c) -> p i c", i=4)
                        pt, pk = pts.pop(si)
                        pi = ptc[0] % 3
                        ptc[0] += 1
                        cx.op("act", lambda e, pt=pt, pi=pi: e.activation(out=PT[pi][:], in_=pt[:], func=AF.Exp),
                              reads=[pk], writes=[("PT", pi)])
                        nv = 97 if br == 0 else 65
                        vap = vcx[:, 0:97] if br == 0 else vsw[:, kb, br - 1, :]
                        for i in range(4):
                            cx.op("pe", lambda e, i=i, pi=pi, vap=vap, nv=nv, pov=pov, ii=ii, kbs=kbs: e.matmul(
                                pov[:, i, 0:nv], PT[pi][:, i * 128:(i + 1) * 128], vap,
                                start=(ii == 0 and i == 0), stop=(ii == len(kbs) - 1 and i == 3)),
                                  reads=[("PT", pi), "vcx"], writes=[pok])
                        if ii == len(kbs) - 1:
                            rd, fc, to = rden[oi], fac[oi], tmpo[oi]
                            if br == 0:
                                cx.op("dve", lambda e, rd=rd, pov=pov: e.tensor_scalar_add(out=rd[:], in0=pov[:, :, 64], scalar1=1e-30), reads=[pok], writes=[("rden", oi)])
                                cx.op("dve", lambda e, rd=rd: e.reciprocal(out=rd[:], in_=rd[:]), reads=[("rden", oi)], writes=[("rden", oi)])
                            else:
                                cx.op("dve", lambda e, rd=rd, pov=pov: e.reciprocal(out=rd[:], in_=pov[:, :, 64]), reads=[pok], writes=[("rden", oi)])
                            if br == 0 and qb >= 8:
                                cx.op("dve", lambda e, rd=rd, pov=pov: e.tensor_tensor(out=tmpi[:], in0=pov[:, :, 65:97], in1=rd[:].unsqueeze(2).to_broadcast([128, 4, 32]), op=ALU.mult),
                                      reads=[pok, ("rden", oi)], writes=["tmpi"])
                                cx.op("dve", lambda e, hg=hg: e.tensor_reduce(out=imph[:, hg, :], in_=tmpi[:].rearrange("p h s -> p s h"), axis=AX.X, op=ALU.add),
                                      reads=["tmpi"], writes=[("imph", hg)])
                            cx.op("dve", lambda e, rd=rd, fc=fc, hs=hs, par=par, br=br: e.tensor_tensor(out=fc[:], in0=rd[:], in1=gs_v[:, hs, par, br], op=ALU.mult),
                                  reads=[("rden", oi), ("gsig", cur)], writes=[("fac", oi)])
                            av = acc_v[:, hs, par, :]
                            facb = fc[:].unsqueeze(2).to_broadcast([128, 4, 64])
                            if br == 0:
                                cx.op("dve", lambda e, av=av, pov=pov, facb=facb: e.tensor_tensor(out=av, in0=pov[:, :, 0:64], in1=facb, op=ALU.mult),
                                      reads=[pok, ("fac", oi)], writes=[("acc", hg)])
                            else:
                                cx.op("dve", lambda e, to=to, pov=pov, facb=facb: e.tensor_tensor(out=to[:], in0=pov[:, :, 0:64], in1=facb, op=ALU.mult),
                                      reads=[pok, ("fac", oi)], writes=[("tmpo", 0)])
                                cx.op("pool", lambda e, av=av, to=to: e.tensor_tensor(out=av, in0=av, in1=to[:], op=ALU.add),
                                      reads=[("tmpo", 0), ("acc", hg)], writes=[("acc", hg)])
                            if u == 3 and qb >= 8:
                                topk_dve(qb)
                            while pending and pending[0][0] <= u:
                                pending.pop(0)[1]()
                            if u == 5 and qb + 1 < NT:
                                prep_a(qb + 1)
                            if u == 7 and qb + 1 < NT:
                                prep(qb + 1)
                    cx.op("pool", lambda e: e.tensor_copy(out=ob[:], in_=acc_o[:]), reads=[("acc", g) for g in range(4)], writes=["ob"])

                pending = []

                def tail_stages(qb):
                    ti = qb
                    ysrc = [(ps_y[0][:, 0:512], ("yh", 0)), (ps_y[0][:, 512:1024], ("yh", 1))]

                    def T1():
                        for k in range(8):
                            cx.op("pe", lambda e, k=k: e.transpose(ps_t[0][:, k * 128:(k + 1) * 128], ob[:, k * 128:(k + 1) * 128], ident_bf[:]),
                                  reads=["ob", "ident_bf"], writes=["ps_t"])
                        cx.op("act", lambda e: e.copy(out=ocT[:], in_=ps_t[0][:].rearrange("p (k n) -> p k n", k=8)), reads=["ps_t"], writes=["ocT"])

                    def T2():
                        for nh in range(2):
                            yap, yk = ysrc[nh]
                            for c in range(10):
                                lh = ocT[:, c, :] if c < 8 else ypT[:, c - 8, ti * 128:(ti + 1) * 128]
                                cx.op("pe", lambda e, nh=nh, c=c, lh=lh, yap=yap: e.matmul(yap, lh, wo[:, c, nh * 512:(nh + 1) * 512],
                                                                                        start=(c == 0), stop=(c == 9)),
                                      reads=["ocT", "wo"], writes=[yk])

                    def T3a():
                        deepnorm_a(ti, zt, stats, mv, ysrc)

                    def T3b():
                        deepnorm_b(mv, rstd)

                    def T3c():
                        deepnorm_c(ti, zt, mv, rstd, nmr, aux="pool", dve_norm=True)
                    return [(3, T1), (4, T2), (6, T3a), (8, T3b), (9, T3c)]

                prep_a(0)
                prep(0)
                for qb in range(NT):
                    attention(qb)
                    while pending:
                        pending.pop(0)[1]()
                    pending.extend(tail_stages(qb))
                while pending:
                    pending.pop(0)[1]()
                cx.barrier()

    build_tables()
    done = False
    for b in range(nseq):
        for t0 in range(0, NT, 4):
            cx.dma("sp", x_sb[:, t0:t0 + 4, :], x_d[b, t0 * 128:(t0 + 4) * 128, :].rearrange("(t p) d -> p t d", p=128),
                   writes=[("x", t) for t in range(t0, t0 + 4)])
        for l in range(DEPTH):
            for sub in range(3):
                if sub == 0:
                    ffn(l, 0, 0, b)
                elif sub == 2:
                    ffn(l, 1, 2, b)
                elif l % 2 == 0:
                    even(l, b)
                else:
                    odd(l, b)
                if stop_after == (l, sub):
                    done = True
                    break
            if done:
                break
        for t0 in range(0, NT, 4):
            cx.dma("sp", out_d[b, t0 * 128:(t0 + 4) * 128, :].rearrange("(t p) d -> p t d", p=128), x_sb[:, t0:t0 + 4, :],
                   reads=[("x", t) for t in range(t0, t0 + 4)], writes=[("out", b, t0)])
        cx.barrier()
    es.close()
    return nc


def t5_bucket_np(dist):
    n = np.maximum(dist, 0)
    nf = np.maximum(n, 1).astype(np.float32)
    large = 16 + (np.log(nf / np.float32(16)) / np.float32(math.log(128 / 16)) * np.float32(16)).astype(np.int32)
    return np.where(n < 16, n, np.minimum(large, 31))


def struct_consts():
    c = {}
    ohp = np.zeros((33, 384), np.float32)
    for i in range(384):
        dist = i - 127
        if dist < 0:
            ohp[32, i] = 1.0
        else:
            ohp[int(t5_bucket_np(np.array(dist))), i] += 1.0
            ohp[31, i] -= 1.0
    c["ohp"] = ohp
    sh = np.zeros((17, 248), np.float32)
    for r in range(16):
        sh[r, r + 111] = 1.0
    sh[16, 127:] = 1.0
    c["shiftbig"] = sh
    jl = np.arange(128)[:, None]
    tl = np.arange(128)[None, :]
    mlt = np.where(jl > tl, 0.0, NEGB).astype(np.float32)
    c["mlt4"] = np.ascontiguousarray(np.tile(mlt, (1, 4)))
    bi = np.zeros((32, 2048), np.float32)
    for s_ in range(32):
        bi[s_, s_ * 64:(s_ + 1) * 64] = 32768.0
    c["bi"] = bi
    ab = np.zeros((128, 64), np.float32)
    bb = np.zeros((128, 64), np.float32)
    for t in range(128):
        cur = 1 if t >= 64 else 0
        for sp in range(64):
            s2 = sp - 32
            if s2 == cur:
                bb[t, sp] = 2e30
            elif s2 == cur - 1:
                bb[t, sp] = 1e30
            elif s2 > cur:
                bb[t, sp] = -1e30
            else:
                ab[t, sp] = 1.0
    c["abig"], c["bbig"] = ab, bb
    ov = np.zeros((128, 32), np.float32)
    for n in range(127):
        st_, en = 16 * n, 16 * n + 32
        for s_ in range(32):
            o = min(en, 64 * s_ + 64) - max(st_, 64 * s_)
            if o > 0:
                ov[n, s_] = o / 32.0
    c["ov"] = ov
    cw = np.ones((128, 2, 16), np.float32)
    for p in range(128):
        for ch in range(2):
            w = (2, 4, 8, 16)[ch * 2 + p // 64]
            for t in range(16):
                cw[p, ch, t] = w / min(t + 1, w)
    c["corrw"] = cw
    return c


def make_in_maps(inputs, ncores=8):
    f = lambda a: np.ascontiguousarray(np.asarray(a, dtype=np.float32))
    shared = {k: f(inputs[k]) for k in ("ada_w", "ada_b", "ln_g", "ln_b", "ffn_w_gate", "ffn_w_up", "ffn_w_down",
                                        "ev_w_in", "ev_conv_a_b", "ev_norm_a_g", "ev_norm_a_b", "ev_w_out")}
    shared["ev_conv_a_wT"] = f(np.transpose(inputs["ev_conv_a_w"], (0, 2, 1)))
    shared["ev_conv_b_wT"] = f(np.transpose(inputs["ev_conv_b_w"], (0, 2, 1)))
    shared["ident"] = np.eye(128, dtype=np.float32)
    for k in ("od_w_in", "od_cmp_w1_k", "od_cmp_w1_v", "od_cmp_w2_k", "od_cmp_w2_v", "od_pool_w", "od_pool_scale", "od_w_out"):
        shared[k] = f(inputs[k])
    shared["od_peT"] = f(np.concatenate([np.transpose(inputs["od_cmp_pe_k"], (0, 2, 1)), np.transpose(inputs["od_cmp_pe_v"], (0, 2, 1))], axis=1))
    shared["rbx"] = f(np.concatenate([np.asarray(inputs["rel_bias"], np.float32), np.full((1, 16), NEGB, np.float32)], axis=0))
    shared.update(struct_consts())
    x = f(inputs["x"])
    c = f(inputs["c"])
    maps = []
    for i in range(ncores):
        m = dict(shared)
        m["x"] = x[2 * i:2 * i + 2]
        m["cT"] = f(c[2 * i:2 * i + 2].T)
        maps.append(m)
    return maps


def kernel(**inputs):
    nc = build()
    maps = make_in_maps(inputs)
    res = run_bass_kernel_spmd(nc, maps, core_ids=list(range(8)))
    return np.concatenate([r["out"] for r in res.results], axis=0)
```
